# Optimizing a Trainium2 kernel written in Bass

```python
import math
import jax, jax.numpy as jnp
from jax import lax
import numpy as np

D_MODEL = 4096
BATCH = 4
SEQ = 2048
DEPTH = 1

MIX_WIDTH = D_MODEL
W_A = MIX_WIDTH // 2
W_B = MIX_WIDTH - W_A
CHUNK = 128
A_HEADS = 16
A_HEAD_DIM = W_A // A_HEADS
HEAD_DIM = 128
B_HEADS = W_B // HEAD_DIM
B_KV_HEADS = 4
B_GROUP = B_HEADS // B_KV_HEADS
IDX_HEADS = 32
IDX_DIM = 128
TOPK_MAX = 256
Q_BLOCK = 128
NUM_BUCKETS = 32
MAX_DISTANCE = 128
EPS = 1e-6
NEG = -1e30

IN_SPLITS = (W_A, W_A, W_A,
             W_B, B_KV_HEADS * HEAD_DIM, B_KV_HEADS * HEAD_DIM, W_B,
             IDX_HEADS * IDX_DIM, IDX_DIM, IDX_HEADS)
N_IN = sum(IN_SPLITS)

kernel_name = "hybrid_gmlp_dsa_adaln_layer"


def rms_norm(x, g):
    xf = x.astype(jnp.float32)
    y = xf * lax.rsqrt(jnp.mean(xf * xf, axis=-1, keepdims=True) + EPS)
    return (y * g.astype(jnp.float32)).astype(x.dtype)


def layer_norm(x, g, b):
    xf = x.astype(jnp.float32)
    mu = jnp.mean(xf, axis=-1, keepdims=True)
    var = jnp.mean(jnp.square(xf - mu), axis=-1, keepdims=True)
    y = (xf - mu) * lax.rsqrt(var + EPS)
    return (y * g.astype(jnp.float32) + b.astype(jnp.float32)).astype(x.dtype)


def t5_bucket(dist):
    max_exact = NUM_BUCKETS // 2
    n = jnp.maximum(dist, 0)
    nf = jnp.maximum(n, 1).astype(jnp.float32)
    large = max_exact + (jnp.log(nf / max_exact) / math.log(MAX_DISTANCE / max_exact)
                         * (NUM_BUCKETS - max_exact)).astype(jnp.int32)
    large = jnp.minimum(large, NUM_BUCKETS - 1)
    return jnp.where(n < max_exact, n, large)


def chunked_sgu(u, v, ln_v_g, ln_v_b, w_s, b_s):
    B, S, _ = u.shape
    u = jax.nn.gelu(u)
    v = layer_norm(jax.nn.gelu(v), ln_v_g, ln_v_b)
    n_chunks = S // CHUNK
    v = v.reshape(B, n_chunks, CHUNK, A_HEADS, A_HEAD_DIM)
    u = u.reshape(B, n_chunks, CHUNK, A_HEADS, A_HEAD_DIM)
    ws = w_s * jnp.tril(jnp.ones((CHUNK, CHUNK), w_s.dtype))[None]
    sp = jnp.einsum('hts,bnshc->bnthc', ws, v) + b_s.T[None, None, :, :, None]
    return (u * sp).reshape(B, S, W_A)


def sparse_attention(q, k, v, q_idx, k_idx, w_idx, rel_bias, k_sel):
    B, S = q.shape[:2]
    n_blocks = S // Q_BLOCK
    s_pos = jnp.arange(S)
    qg = q.reshape(B, S, B_KV_HEADS, B_GROUP, HEAD_DIM)
    gather = jax.vmap(lambda seq, ids: seq[ids])
    k_idx32 = k_idx.astype(jnp.float32)

    def block(i):
        t0 = i * Q_BLOCK
        qb = lax.dynamic_slice_in_dim(qg, t0, Q_BLOCK, axis=1)
        qib = lax.dynamic_slice_in_dim(q_idx, t0, Q_BLOCK, axis=1)
        wib = lax.dynamic_slice_in_dim(w_idx, t0, Q_BLOCK, axis=1)
        t_pos = t0 + jnp.arange(Q_BLOCK)
        dots = jnp.einsum('bqhd,bsd->bqhs', qib.astype(jnp.float32), k_idx32)
        score = jnp.einsum('bqh,bqhs->bqs', wib.astype(jnp.float32), jax.nn.relu(dots))
        causal = s_pos[None, :] <= t_pos[:, None]
        score = jnp.where(causal[None], score, -jnp.inf)
        _, sel = lax.top_k(score, k_sel)
        valid = sel <= t_pos[None, :, None]
        ks = gather(k, sel)
        vs = gather(v, sel)
        logits = jnp.einsum('bqgrd,bqkgd->bqgrk', qb, ks).astype(jnp.float32) * (HEAD_DIM ** -0.5)
        bias = rel_bias[t5_bucket(t_pos[None, :, None] - sel)].astype(jnp.float32)
        bias = bias.reshape(B, Q_BLOCK, k_sel, B_KV_HEADS, B_GROUP).transpose(0, 1, 3, 4, 2)
        logits = jnp.where(valid[:, :, None, None, :], logits + bias, NEG)
        p = jax.nn.softmax(logits, axis=-1).astype(vs.dtype)
        o = jnp.einsum('bqgrk,bqkgd->bqgrd', p, vs)
        return o.reshape(B, Q_BLOCK, W_B)

    out = lax.map(block, jnp.arange(n_blocks))
    return out.transpose(1, 0, 2, 3).reshape(B, S, W_B)


def hybrid_layer(x, c, w_ada, b_ada, norm_g, w_in, ln_v_g, ln_v_b, w_s, b_s,
                 q_norm_g, k_norm_g, out_norm_a_g, out_norm_b_g, w_out, rel_bias, k_sel):
    B, S, D = x.shape
    mod = jax.nn.silu(c) @ w_ada + b_ada
    shift, scale, gate = jnp.split(mod, 3, axis=-1)
    h = rms_norm(x, norm_g) * (1 + scale[:, None, :]) + shift[:, None, :]
    z = h @ w_in
    offs = np.cumsum(IN_SPLITS)[:-1].tolist()
    uA, vA, gA, qB, kB, vB, gB, qI, kI, wI = jnp.split(z, offs, axis=-1)
    yA = chunked_sgu(uA, vA, ln_v_g, ln_v_b, w_s, b_s)
    q = rms_norm(qB.reshape(B, S, B_HEADS, HEAD_DIM), q_norm_g)
    k = rms_norm(kB.reshape(B, S, B_KV_HEADS, HEAD_DIM), k_norm_g)
    v = vB.reshape(B, S, B_KV_HEADS, HEAD_DIM)
    q_idx = qI.reshape(B, S, IDX_HEADS, IDX_DIM)
    w_idx = wI * (IDX_HEADS ** -0.5 * IDX_DIM ** -0.5)
    yB = sparse_attention(q, k, v, q_idx, kI, w_idx, rel_bias, k_sel)
    yA = rms_norm(yA, out_norm_a_g) * jax.nn.silu(gA)
    yB = rms_norm(yB, out_norm_b_g) * jax.nn.silu(gB)
    y = jnp.concatenate([yA, yB], axis=-1) @ w_out
    return x + gate[:, None, :] * y


def setup_inputs(seed: int = 0) -> dict:
    key = jax.random.key(seed)
    ks = jax.random.split(key, 16)
    f = jnp.float32
    nrm = lambda k, shape: jax.random.normal(k, shape, f)
    return {
        "x": nrm(ks[0], (BATCH, SEQ, D_MODEL)),
        "c": nrm(ks[1], (BATCH, D_MODEL)),
        "w_ada": nrm(ks[2], (DEPTH, D_MODEL, 3 * D_MODEL)) * (0.5 * D_MODEL ** -0.5),
        "b_ada": nrm(ks[3], (DEPTH, 3 * D_MODEL)) * 0.01,
        "norm_g": 1.0 + 0.02 * nrm(ks[4], (DEPTH, D_MODEL)),
        "w_in": nrm(ks[5], (DEPTH, D_MODEL, N_IN)) * D_MODEL ** -0.5,
        "ln_v_g": 1.0 + 0.02 * nrm(ks[6], (DEPTH, W_A)),
        "ln_v_b": 0.02 * nrm(ks[7], (DEPTH, W_A)),
        "w_s": nrm(ks[8], (DEPTH, A_HEADS, CHUNK, CHUNK)) * (0.5 * CHUNK ** -0.5),
        "b_s": 1.0 + 0.1 * nrm(ks[9], (DEPTH, A_HEADS, CHUNK)),
        "q_norm_g": 1.0 + 0.02 * nrm(ks[10], (DEPTH, HEAD_DIM)),
        "k_norm_g": 1.0 + 0.02 * nrm(ks[11], (DEPTH, HEAD_DIM)),
        "rel_bias": 0.5 * nrm(ks[12], (NUM_BUCKETS, B_HEADS)),
        "out_norm_a_g": 1.0 + 0.02 * nrm(ks[13], (DEPTH, W_A)),
        "out_norm_b_g": 1.0 + 0.02 * nrm(ks[14], (DEPTH, W_B)),
        "w_out": nrm(ks[15], (DEPTH, MIX_WIDTH, D_MODEL)) * MIX_WIDTH ** -0.5,
    }


def reference(x, c, w_ada, b_ada, norm_g, w_in, ln_v_g, ln_v_b, w_s, b_s,
              q_norm_g, k_norm_g, rel_bias, out_norm_a_g, out_norm_b_g, w_out):
    k_sel = min(TOPK_MAX, x.shape[1] // 4)
    for l in range(DEPTH):
        x = hybrid_layer(x, c, w_ada[l], b_ada[l], norm_g[l], w_in[l], ln_v_g[l], ln_v_b[l],
                         w_s[l], b_s[l], q_norm_g[l], k_norm_g[l], out_norm_a_g[l],
                         out_norm_b_g[l], w_out[l], rel_bias, k_sel)
    return x
```

```python
import os
import math
import numpy as np
import ml_dtypes
import concourse.bass as bass
import concourse.mybir as mybir
from concourse.bass_utils import run_bass_kernel_spmd
from contextlib import ExitStack

F32 = mybir.dt.float32
BF16 = mybir.dt.bfloat16
ALU = mybir.AluOpType
AF = mybir.ActivationFunctionType
AX = mybir.AxisListType

EPS = 1e-6
NIT = 22
NEG = -1.0e30
ARENA_F32 = int(os.environ.get("K_ARENA", "53200"))
C_U, C_V, C_GA, C_Q, C_K, C_VB, C_GB, C_QI, C_KI, C_WI = 0, 2048, 4096, 6144, 8192, 8704, 9216, 11264, 15360, 15488


class _Op:
    __slots__ = ("eng", "fn", "deps", "sem", "inc", "val", "needed", "idx", "phase")

    def __init__(self, eng, fn, deps, sem, inc, idx):
        self.eng, self.fn, self.deps, self.sem, self.inc, self.idx = eng, fn, deps, sem, inc, idx
        self.val = None
        self.needed = False


class Sched:
    ENGS = ("pe", "act", "dve", "pool", "sp")

    def __init__(self, nc):
        self.nc = nc
        self.ops = {e: [] for e in self.ENGS}
        self.last_w = {}
        self.readers = {}
        self.ghost = {}
        self.known = set()
        self.chan_names = []
        self.n = 0
        self.phase = "setup"
        self.scopes = os.environ.get("K_SCOPES", "") == "1"

    def chan(self, name):
        self.chan_names.append(name)
        return name

    def retire(self, old, new):
        g = self.ghost.setdefault(new, {})
        for k, w in self.last_w.items():
            if k[0] == old:
                self._merge(g, w)
        for k, r in self.readers.items():
            if k[0] == old:
                for o in r.values():
                    self._merge(g, o)
        for o in self.ghost.get(old, {}).values():
            self._merge(g, o)

    def forget(self, name):
        g = {}
        for k in [k for k in self.last_w if k[0] == name]:
            self._merge(g, self.last_w.pop(k))
        for k in [k for k in self.readers if k[0] == name]:
            for o in self.readers.pop(k).values():
                self._merge(g, o)
        for o in self.ghost.get(name, {}).values():
            self._merge(g, o)
        self.ghost[name] = g
        self.known = {k for k in self.known if k[0] != name}

    @staticmethod
    def _merge(d, op):
        cur = d.get(op.sem)
        if cur is None or cur.idx < op.idx:
            d[op.sem] = op

    def _add(self, eng, fn, reads, writes, sem, inc):
        deps = {}
        for k in list(reads) + list(writes):
            if k not in self.known:
                self.known.add(k)
                for o in self.ghost.get(k[0], {}).values():
                    self._merge(deps, o)
            w = self.last_w.get(k)
            if w is not None:
                self._merge(deps, w)
        for k in writes:
            for o in self.readers.get(k, {}).values():
                self._merge(deps, o)
        self.n += 1
        op = _Op(eng, fn, list(deps.values()), sem, inc, self.n)
        op.phase = self.phase
        for k in reads:
            self._merge(self.readers.setdefault(k, {}), op)
        for k in writes:
            self.last_w[k] = op
            self.readers[k] = {}
        self.ops[eng].append(op)
        return op

    def op(self, eng, fn, reads=(), writes=()):
        return self._add(eng, fn, reads, writes, eng, 1)

    def dma(self, q, chan, out, in_, reads=(), writes=()):
        return self._add(q, lambda e: e.dma_start(out=out, in_=in_), reads, writes, chan, 16)

    def emit(self, final_waits=()):
        nc = self.nc
        for e in self.ENGS:
            for op in self.ops[e]:
                for d in op.deps:
                    if d.eng == "pe" and op.eng == "pe" and d.sem == "pe":
                        continue
                    d.needed = True
        for op in final_waits:
            op.needed = True
        for e in self.ENGS:
            for op in self.ops[e]:
                if op.inc == 16:
                    op.needed = True
        counters = {}
        for e in self.ENGS:
            for op in self.ops[e]:
                if op.needed:
                    counters[op.sem] = counters.get(op.sem, 0) + op.inc
                    op.val = counters[op.sem]
        sem_keys = list(self.ENGS) + self.chan_names
        with ExitStack() as st:
            sems = {k: st.enter_context(nc.semaphore("s_" + k)) for k in sem_keys}
            block = st.enter_context(nc.Block())
            engmap = {"pe": block.tensor, "act": block.scalar, "dve": block.vector,
                      "pool": block.gpsimd, "sp": block.sync}

            def make(ename):
                def body(eng):
                    seen = {}
                    cur = [None, None]
                    for op in self.ops[ename]:
                        if self.scopes and op.phase != cur[0]:
                            if cur[1] is not None:
                                cur[1].__exit__(None, None, None)
                            cur[1] = nc.named_scope(op.phase)
                            cur[1].__enter__()
                            cur[0] = op.phase
                        need = {}
                        for d in op.deps:
                            if d.eng == "pe" and ename == "pe" and d.sem == "pe":
                                continue
                            if need.get(d.sem, 0) < d.val:
                                need[d.sem] = d.val
                        for k, v in need.items():
                            if seen.get(k, 0) < v:
                                eng.wait_ge(sems[k], v)
                                seen[k] = v
                        ins = op.fn(eng)
                        if op.needed:
                            ins.then_inc(sems[op.sem], op.inc)
                    if cur[1] is not None:
                        cur[1].__exit__(None, None, None)
                    if ename == "sp":
                        for op in final_waits:
                            if seen.get(op.sem, 0) < op.val:
                                eng.wait_ge(sems[op.sem], op.val)
                                seen[op.sem] = op.val
                return body

            for ename in self.ENGS:
                engmap[ename](make(ename))


class StopBuild(Exception):
    pass


class Arena:
    def __init__(self, nc, S, st, nfloats):
        self.S = S
        self.t = st.enter_context(nc.sbuf_tensor("arena", [128, nfloats], F32))
        self.n = nfloats
        self.live = {}
        self.dead = []

    def alloc(self, name, dtype, shape, top=False):
        nel = int(np.prod(shape))
        nf = nel if dtype == F32 else (nel + 1) // 2
        nf = (nf + 15) // 16 * 16
        spans = sorted((o, o + n) for (o, n) in self.live.values())
        if top:
            pos = self.n - nf
            for (a, b) in reversed(spans):
                if b <= pos:
                    break
                pos = min(pos, a - nf)
            assert pos >= 0
        else:
            pos = 0
            for (a, b) in spans:
                if a - pos >= nf:
                    break
                pos = max(pos, b)
        assert pos + nf <= self.n, f"SBUF arena overflow allocating {name} ({nf} floats), live={self.live}"
        self.live[name] = (pos, nf)
        self.S.forget(name)
        for (dn, do, dnf) in self.dead:
            if do < pos + nf and pos < do + dnf:
                self.S.retire(dn, name)
        ap = self.t[:, pos:pos + nf]
        if dtype != F32:
            ap = ap.bitcast(dtype)
        ap = ap[:, 0:nel]
        if len(shape) == 2:
            ap = ap.rearrange("p (a b) -> p a b", a=shape[0], b=shape[1])
        elif len(shape) == 3:
            ap = ap.rearrange("p (a b c) -> p a b c", a=shape[0], b=shape[1], c=shape[2])
        return ap

    def free(self, *names):
        for name in names:
            o, n = self.live.pop(name)
            self.dead.append((name, o, n))


def build_nc(dbg=False):
    nc = bass.Bass("TRN2", target_bir_lowering=False)

    def D(name, shape, kind="ExternalInput", dt=F32):
        return nc.dram_tensor(name, shape, dt, kind=kind).ap()

    x_d = D("x", [2048, 4096]); cT_d = D("cT", [128, 32]); wada_d = D("w_ada", [4096, 12288])
    badaT_d = D("b_adaT", [128, 96]); ngT_d = D("norm_gT", [128, 32]); win_d = D("w_in", [4096, 15520])
    lnv_d = D("lnvT", [128, 32]); ws_d = D("w_s", [16, 128, 128]); bs_d = D("b_s", [1, 2048])
    qkg_d = D("qk_g", [128, 2]); rb_d = D("rel_bias", [32, 16]); on_d = D("onT", [128, 32])
    wout_d = D("w_out", [4096, 4096])
    ident_d = D("ident", [128, 128]); trilT_d = D("trilT", [128, 128]); cmask_d = D("cmask", [128, 256])
    J_d = D("Jm", [128, 128]); oh_d = D("ohrev", [32, 383]); flag_d = D("flag0", [128, 1])
    out_d = D("out", [1024, 4096], kind="ExternalOutput")
    scr_t = nc.dram_tensor("scr", [16, 383], F32)
    scr_d = scr_t.ap()
    dbg_d = {}

    STOP = os.environ.get("K_STOP", "")

    def ckpt(name):
        if STOP == name:
            raise StopBuild()

    with ExitStack() as st:
        S = Sched(nc)
        A = Arena(nc, S, st, ARENA_F32)
        try:
            _build_body(nc, S, A, st, D, dbg, dbg_d, ckpt, x_d, cT_d, wada_d, badaT_d, ngT_d, win_d, lnv_d, ws_d, bs_d, qkg_d, rb_d,
                        on_d, wout_d, ident_d, trilT_d, cmask_d, J_d, oh_d, flag_d, out_d, scr_t, scr_d)
        except StopBuild:
            S.emit(final_waits=list(dbg_d.values()))
    return nc


def _build_body(nc, S, A, st, D, dbg, dbg_d, ckpt, x_d, cT_d, wada_d, badaT_d, ngT_d, win_d, lnv_d, ws_d, bs_d, qkg_d, rb_d,
                on_d, wout_d, ident_d, trilT_d, cmask_d, J_d, oh_d, flag_d, out_d, scr_t, scr_d):
    if True:
        pb = [st.enter_context(nc.psum_tensor(f"pb{i}", [128, 512], F32)) for i in range(8)]

        def PS(i):
            return pb[i][:]

        def PSB(i):
            return pb[i][:].bitcast(BF16)

        def pk(i):
            return ("ps%d" % i,)

        _ring = {}

        def ring(name, n):
            i = _ring.get(name, 0)
            _ring[name] = i + 1
            return i % n

        def dbg_out(name, ap_sb, shape, reads, dt=F32):
            if not dbg:
                return None
            d = D("dbg_" + name, shape, kind="ExternalOutput", dt=dt)
            dbg_d[name] = S.dma("sp", S.chan("dbg_" + name), d, ap_sb, reads=reads)
            return d

        identf = A.alloc("identf", F32, [128]); identb = A.alloc("identb", BF16, [128])
        onesb = A.alloc("onesb", BF16, [128]); onesf = A.alloc("onesf", F32, [128])
        Jf = A.alloc("Jf", F32, [128]); trilT = A.alloc("trilT", F32, [128]); cmask = A.alloc("cmask", F32, [256])
        cT = A.alloc("cT", F32, [32]); badaT = A.alloc("badaT", F32, [96]); ngT = A.alloc("ngT", F32, [32])
        lnvT = A.alloc("lnvT", F32, [32]); onT = A.alloc("onT", F32, [32]); qkg = A.alloc("qkg", F32, [16])[:, 0:2]
        flag0 = A.alloc("flag0", F32, [16])[:, 0:1]
        sme = A.alloc("sme", F32, [256])
        gqs = sme[:, 0:1]; c31 = sme[:, 1:2]; epsc = sme[:, 2:3]; epsc128 = sme[:, 3:4]
        modT = sme[:, 16:112]; a_sc = sme[:, 112:144]
        S.op("dve", lambda e: e.memset(epsc, EPS), writes=[("epsc",)])
        S.op("dve", lambda e: e.memset(epsc128, 128.0 * EPS), reads=[("epsc",)], writes=[("epsc",)])
        ldc = S.chan("ldc")
        small_loads = [(identf, ident_d, "identf"), (Jf, J_d, "Jf"), (trilT, trilT_d, "trilT"), (cmask, cmask_d, "cmask"),
                       (cT, cT_d, "cT"), (badaT, badaT_d, "badaT"), (ngT, ngT_d, "ngT"), (lnvT, lnv_d, "lnvT"),
                       (onT, on_d, "onT"), (qkg, qkg_d, "qkg"), (flag0, flag_d, "flag0")]
        for (dst, src, nm) in small_loads:
            S.dma("sp", S.chan("ld_" + nm), dst, src, writes=[(nm,)])
        S.op("dve", lambda e: e.tensor_copy(out=identb, in_=identf), reads=[("identf",)], writes=[("identb",)])
        S.op("dve", lambda e: e.memset(onesb, 1.0), writes=[("onesb",)])
        S.op("dve", lambda e: e.memset(onesf, 1.0), writes=[("onesf",)])
        S.op("dve", lambda e: e.tensor_scalar(out=gqs, in0=qkg[:, 0:1], scalar1=math.sqrt(128.0), scalar2=None,
                                              op0=ALU.mult), reads=[("qkg",)], writes=[("gqs",)])

        S.phase = "adaln"
        scb = A.alloc("scb", BF16, [32])
        S.op("act", lambda e: e.activation(out=scb, in_=cT, func=AF.Silu), reads=[("cT",)], writes=[("scb",)])
        wt = A.alloc("wt", BF16, [3, 32, 128])
        wch = [S.chan(f"wch{i}") for i in range(3)]
        wada_v = wada_d.rearrange("(kc p) n -> p kc n", p=128)
        win_v = win_d.rearrange("(kc p) n -> p kc n", p=128)

        def load_w(view, col0, ncols):
            b = ring("wt", 3)
            S.dma("pool", wch[b], wt[:, b, :, 0:ncols], view[:, :, col0:col0 + ncols], writes=[("wt", b)])
            return b

        for fcg in range(12):
            for kcg in range(8):
                b = ring("wt", 3)
                tile_ap = wt[:, b, :, :].rearrange("p a b -> p (a b)").rearrange("p (k n) -> p k n", k=4)
                S.dma("pool", wch[b], tile_ap, wada_v[:, kcg * 4:(kcg + 1) * 4, fcg * 1024:(fcg + 1) * 1024], writes=[("wt", b)])
                for f8 in range(8):
                    fc = fcg * 8 + f8
                    bank = 3 if fc < 64 else 4
                    col = (fc % 64) * 8 + kcg
                    for kk in range(4):
                        kc = kcg * 4 + kk
                        S.op("pe", lambda e, tile_ap=tile_ap, kk=kk, f8=f8, kc=kc, bank=bank, col=col: e.matmul(
                            PS(bank)[:, col:col + 1], lhsT=tile_ap[:, kk, f8 * 128:(f8 + 1) * 128], rhs=scb[:, kc:kc + 1],
                            start=(kk == 0), stop=(kk == 3)),
                            reads=[("wt", b), ("scb",)], writes=[pk(bank)])
        S.op("dve", lambda e: e.tensor_reduce(out=modT[:, 0:64], in_=PS(3).rearrange("p (f k) -> p f k", k=8), axis=AX.X, op=ALU.add),
             reads=[pk(3)], writes=[("modT",)])
        S.op("dve", lambda e: e.tensor_reduce(out=modT[:, 64:96], in_=PS(4)[:, 0:256].rearrange("p (f k) -> p f k", k=8), axis=AX.X, op=ALU.add),
             reads=[pk(4), ("modT",)], writes=[("modT",)])
        S.op("dve", lambda e: e.tensor_tensor(out=modT, in0=modT, in1=badaT, op=ALU.add),
             reads=[("modT",), ("badaT",)], writes=[("modT",)])
        shiftT = modT[:, 0:32]; scaleT = modT[:, 32:64]; gateT = modT[:, 64:96]
        S.op("dve", lambda e: e.scalar_tensor_tensor(out=a_sc, in0=scaleT, scalar=1.0, in1=ngT, op0=ALU.add, op1=ALU.mult),
             reads=[("modT",), ("ngT",)], writes=[("a_sc",)])
        A.free("scb")
        dbg_out("modT", modT, [128, 96], [("modT",)])
        ckpt("ada")

        xch = [S.chan(f"xch{i}") for i in range(2)]

        XPL = int(os.environ.get("K_XP", "9"))
        def x_pass(tok0, hT):
            xt = A.alloc("xt", F32, [2, 4096]); xn = A.alloc("xn", BF16, [2, 4096])
            st1 = A.alloc("st1", F32, [32])
            S.op("dve", lambda e: e.memset(st1, 0.0), writes=[("st1", q) for q in range(24)])
            for t in range(8):
                b = ring("xt", 2)
                nb = ring("xn", 2)
                S.dma("sp", xch[b], xt[:, b, :], x_d[tok0 + t * 128: tok0 + (t + 1) * 128, :], writes=[("xt", b)])
                S.op("act", lambda e, b=b, t=t, nb=nb: e.activation(out=xn[:, nb, :], in_=xt[:, b, :], func=AF.Square,
                                                                    accum_out=st1[:, t:t + 1]),
                     reads=[("xt", b)], writes=[("xn", nb), ("st1", t)])
                if XPL < 2:
                    continue
                S.op("act", lambda e, t=t: e.activation(out=st1[:, 8 + t:9 + t], in_=st1[:, t:t + 1], func=AF.Sqrt, scale=1.0 / 4096, bias=epsc),
                     reads=[("st1", t), ("epsc",)], writes=[("st1", 8 + t)])
                S.op("dve", lambda e, t=t: e.reciprocal(out=st1[:, 16 + t:17 + t], in_=st1[:, 8 + t:9 + t]),
                     reads=[("st1", 8 + t)], writes=[("st1", 16 + t)])
                if XPL < 3:
                    continue
                S.op("act", lambda e, b=b, nb=nb, t=t: e.activation(out=xn[:, nb, :], in_=xt[:, b, :], func=AF.Copy,
                                                                    scale=st1[:, 16 + t:17 + t]),
                     reads=[("xt", b), ("st1", 16 + t)], writes=[("xn", nb)])
                if XPL < 4:
                    continue
                for k8 in range(4):
                    bank = 4 + ring("xtp", 2)
                    for kk in range(8):
                        kc = k8 * 8 + kk
                        S.op("pe", lambda e, bank=bank, kk=kk, kc=kc, nb=nb: e.transpose(
                            out=PSB(bank)[:, kk * 128:(kk + 1) * 128], in_=xn[:, nb, kc * 128:(kc + 1) * 128], identity=identb),
                            reads=[("xn", nb), ("identb",)], writes=[pk(bank)])
                    if XPL < 5:
                        continue
                    for kk in range(8):
                        kc = k8 * 8 + kk
                        src = PSB(bank)[:, kk * 128:(kk + 1) * 128]
                        dst = hT[:, kc, t * 128:(t + 1) * 128]
                        if bank == 4:
                            S.op("dve", lambda e, src=src, dst=dst, kc=kc: e.tensor_scalar(
                                out=dst, in0=src, scalar1=a_sc[:, kc:kc + 1], scalar2=shiftT[:, kc:kc + 1],
                                op0=ALU.mult, op1=ALU.add),
                                reads=[pk(bank), ("a_sc",), ("modT",)], writes=[("hT", kc, t)])
                        else:
                            S.op("act", lambda e, src=src, dst=dst, kc=kc: e.activation(
                                out=dst, in_=src, func=AF.Identity, scale=a_sc[:, kc:kc + 1], bias=shiftT[:, kc:kc + 1]),
                                reads=[pk(bank), ("a_sc",), ("modT",)], writes=[("hT", kc, t)])
            A.free("xt", "xn", "st1")

        deferred = []

        def flush_deferred():
            while deferred:
                deferred.pop(0)()

        def project(hT, col0, ncols, pieces, consumer, inter=None, every=2):
            b = load_w(win_v, col0, ncols)
            for p in pieces:
                bank = (0, 1, 7)[ring("projps", 3)]
                for kc in range(32):
                    S.op("pe", lambda e, b=b, kc=kc, p=p, bank=bank: e.matmul(
                        PS(bank)[0:ncols, :], lhsT=wt[:, b, kc, 0:ncols], rhs=hT[:, kc, p * 512:(p + 1) * 512],
                        start=(kc == 0), stop=(kc == 31)),
                        reads=[("wt", b)] + [("hT", kc, 4 * p + tt) for tt in range(4)], writes=[pk(bank)])
                    if inter and kc % every == every - 1:
                        inter.pop(0)()
                flush_deferred()
                consumer(p, bank)

        kT = A.alloc("kT", BF16, [4, 2048]); Vsb = A.alloc("Vsb", BF16, [16, 512]); kIT = A.alloc("kIT", BF16, [2048])
        tmpb = A.alloc("tmpb", BF16, [3, 512]); tmpf = A.alloc("tmpf", F32, [2, 512])

        def rms_feat(bank, gcol, gkey, dst, dkey, extra_reads=()):
            tb = ring("tmpb", 3)
            S.op("act", lambda e: e.activation(out=tmpb[:, tb, :], in_=PS(bank), func=AF.Square),
                 reads=[pk(bank)], writes=[("tmpb", tb)])
            deferred.append(lambda: rms_feat_b(bank, gcol, gkey, dst, dkey, tb))

        def rms_feat_b(bank, gcol, gkey, dst, dkey, tb):
            S.op("pe", lambda e: e.matmul(PS(2), lhsT=onesb, rhs=tmpb[:, tb, :], start=True, stop=True),
                 reads=[("tmpb", tb), ("onesb",)], writes=[pk(2)])
            fb = ring("tmpf", 2)
            S.op("act", lambda e: e.activation(out=tmpf[:, fb, :], in_=PS(2), func=AF.Sqrt, bias=epsc128),
                 reads=[pk(2), ("epsc",)], writes=[("tmpf", fb)])
            S.op("dve", lambda e: e.reciprocal(out=tmpf[:, fb, :], in_=tmpf[:, fb, :]), reads=[("tmpf", fb)], writes=[("tmpf", fb)])
            S.op("dve", lambda e: e.scalar_tensor_tensor(out=dst, in0=PS(bank).rearrange("p (a b) -> p a b", b=128) if len(dst.shape) == 3 else PS(bank),
                                                         scalar=gcol,
                                                         in1=tmpf[:, fb, :].rearrange("p (a b) -> p a b", b=128) if len(dst.shape) == 3 else tmpf[:, fb, :],
                                                         op0=ALU.mult, op1=ALU.mult),
                 reads=[pk(bank), ("tmpf", fb), gkey], writes=[dkey])

        def kv_pass(hT, L0):
            def c_ki(p, bank):
                S.op("act", lambda e: e.activation(out=kIT[:, L0 + p * 512: L0 + (p + 1) * 512], in_=PS(bank), func=AF.Copy),
                     reads=[pk(bank)], writes=[("kIT", (L0 // 512) + p)])
            project(hT, C_KI, 128, [0, 1], c_ki)
            for g in range(4):
                def c_k(p, bank, g=g):
                    rms_feat(bank, qkg[:, 1:2], ("qkg",), kT[:, g, L0 + p * 512: L0 + (p + 1) * 512], ("kT", g, (L0 // 512) + p))
                project(hT, C_K + g * 128, 128, [0, 1], c_k)
            for g in range(4):
                def c_v(p, bank, g=g):
                    tb = ring("tmpb", 3)
                    S.op("act", lambda e: e.activation(out=tmpb[:, tb, :], in_=PS(bank), func=AF.Copy),
                         reads=[pk(bank)], writes=[("tmpb", tb)])
                    def part_b(tb=tb, p=p, g=g):
                        for tt in range(4):
                            S.op("pe", lambda e, tt=tt: e.transpose(out=PSB(6)[:, tt * 128:(tt + 1) * 128],
                                                                    in_=tmpb[:, tb, tt * 128:(tt + 1) * 128], identity=identb),
                                 reads=[("tmpb", tb), ("identb",)], writes=[pk(6)])
                        Lt = L0 // 128 + p * 4
                        S.op("dve", lambda e: e.tensor_copy(out=Vsb[:, Lt:Lt + 4, g * 128:(g + 1) * 128],
                                                            in_=PSB(6)[:, 0:512].rearrange("p (a b) -> p a b", b=128)),
                             reads=[pk(6)], writes=[("Vsb", g, Lt // 4)])
                    deferred.append(part_b)
                project(hT, C_VB + g * 128, 128, [0, 1], c_v)
            flush_deferred()

        S.phase = "xkv_other"
        hT = A.alloc("hT", BF16, [32, 1024])
        x_pass(1024, hT)
        if os.environ.get("K_STOP", "") == "xp" and XPL >= 5:
            dbg_out("hT", hT, [128, 32, 1024], [("hT", kc, t) for kc in range(32) for t in range(8)], dt=BF16)
        ckpt("xp")
        kv_pass(hT, 1024)
        S.phase = "xkv_own"
        x_pass(0, hT)
        kv_pass(hT, 0)
        if dbg:
            dbg_out("hT", hT, [128, 32, 1024], [("hT", kc, t) for kc in range(32) for t in range(8)], dt=BF16)
            dbg_out("kT", kT, [128, 4, 2048], [("kT", g, p) for g in range(4) for p in range(4)], dt=BF16)
            dbg_out("Vsb", Vsb, [128, 16, 512], [("Vsb", g, p) for g in range(4) for p in range(4)], dt=BF16)
            dbg_out("kIT", kIT, [128, 2048], [("kIT", p) for p in range(4)], dt=BF16)
        ckpt("kv")

        S.phase = "indexer"
        sc = A.alloc("sc", F32, [9216])
        wI = A.alloc("wI", F32, [8, 32]); wItmp = A.alloc("wItmp", F32, [2, 512])

        def c_wi(p, bank):
            S.op("act", lambda e: e.activation(out=wItmp[0:32, p, :], in_=PS(bank)[0:32, :], func=AF.Copy, scale=1.0 / 64.0),
                 reads=[pk(bank)], writes=[("wItmp", p)])
            for tt in range(4):
                S.op("pe", lambda e, tt=tt: e.transpose(out=PS(6)[:, tt * 32:(tt + 1) * 32],
                                                        in_=wItmp[0:32, p, tt * 128:(tt + 1) * 128], identity=identf[0:32, 0:32]),
                     reads=[("wItmp", p), ("identf",)], writes=[pk(6)])
            S.op("dve", lambda e: e.tensor_copy(out=wI[:, p * 4:(p + 1) * 4, :], in_=PS(6)[:, 0:128].rearrange("p (a b) -> p a b", b=32)),
                 reads=[pk(6)], writes=[("wI", p)])
        project(hT, C_WI, 32, [0, 1], c_wi)

        qIT = A.alloc("qIT", BF16, [2, 1024])
        soff = [128 * i * (i + 1) for i in range(8)]
        def dot_thunks(h, qb):
            th = []
            for i in range(8):
                w_i = 128 * (i + 1)
                for part in range(2):
                    for c0 in range(0, w_i, 512):
                        def thunk(i=i, w_i=w_i, part=part, c0=c0):
                            cw = min(512, w_i - c0)
                            bank = 3 + ring("dotps", 2)
                            kcol = part * 1024 + c0
                            S.op("pe", lambda e: e.matmul(
                                PS(bank)[:, 0:cw], lhsT=qIT[:, qb, i * 128:(i + 1) * 128], rhs=kIT[:, kcol:kcol + cw],
                                start=True, stop=True),
                                reads=[("qIT", qb, i // 4)] + [("kIT", q) for q in range(kcol // 512, (kcol + cw - 1) // 512 + 1)],
                                writes=[pk(bank)])
                            tb = ring("tmpb", 3)
                            S.op("act", lambda e: e.activation(out=tmpb[:, tb, 0:cw], in_=PS(bank)[:, 0:cw], func=AF.Relu),
                                 reads=[pk(bank)], writes=[("tmpb", tb)])
                            o0 = soff[i] + part * w_i + c0
                            wcol = wI[:, i, h:h + 1]
                            skey = ("sc", i, part, c0)
                            if h == 0:
                                S.op("dve", lambda e: e.tensor_scalar(
                                    out=sc[:, o0:o0 + cw], in0=tmpb[:, tb, 0:cw], scalar1=wcol, scalar2=None, op0=ALU.mult),
                                    reads=[("tmpb", tb), ("wI", i // 4)], writes=[skey])
                            else:
                                S.op("dve", lambda e: e.scalar_tensor_tensor(
                                    out=sc[:, o0:o0 + cw], in0=tmpb[:, tb, 0:cw], scalar=wcol, in1=sc[:, o0:o0 + cw],
                                    op0=ALU.mult, op1=ALU.add),
                                    reads=[("tmpb", tb), ("wI", i // 4), skey], writes=[skey])
                        th.append(thunk)
            return th

        pending = []
        for h in range(32):
            qb = ring("qIT", 2)

            def c_qi(p, bank, qb=qb):
                S.op("act", lambda e: e.activation(out=qIT[:, qb, p * 512:(p + 1) * 512], in_=PS(bank), func=AF.Copy),
                     reads=[pk(bank)], writes=[("qIT", qb, p)])
            project(hT, C_QI + h * 128, 128, [0, 1], c_qi, inter=pending, every=2)
            while pending:
                pending.pop(0)()
            pending = dot_thunks(h, qb)
        while pending:
            pending.pop(0)()
        sc_keys = [("sc", i, part, c0) for i in range(8) for part in range(2) for c0 in range(0, 128 * (i + 1), 512)]

        def sck(i):
            return [k for k in sc_keys if k[1] == i]
        A.free("qIT", "wItmp", "kIT")
        dbg_out("sc", sc, [128, 9216], sc_keys)
        ckpt("sc")

        S.phase = "topk_qproj"
        qT = A.alloc("qT", BF16, [8, 16, 128])
        junk = A.alloc("junk", BF16, [2048]); tk = A.alloc("tk", F32, [64])
        selv = sc.bitcast(BF16)
        lo = tk[:, 0:8]; w0 = tk[:, 8:16]; mid = tk[:, 16:24]; cnt = tk[:, 24:32]; pw = tk[:, 32:40]
        hi = tk[:, 40:48]; t1 = tk[:, 48:56]; t2 = tk[:, 56:64]

        def q_head(h):
            def c_q(p, bank, h=h):
                rms_feat(bank, gqs, ("gqs",), qT[:, p * 4:(p + 1) * 4, h, :], ("qT", h, p))
            project(hT, C_Q + h * 128, 128, [0, 1], c_q)

        S.op("dve", lambda e: e.memset(tk, 0.0), writes=[("tk",)])
        S.op("dve", lambda e: e.memset(lo[:, 0:1], -1.0e29), reads=[("tk",)], writes=[("tk",)])
        for i in range(8):
            w_i = 128 * (i + 1)
            o = soff[i]
            S.op("dve", lambda e, o=o, w_i=w_i: e.tensor_tensor(out=sc[:, o + w_i - 128:o + w_i], in0=sc[:, o + w_i - 128:o + w_i],
                                                                in1=cmask[:, 0:128], op=ALU.add),
                 reads=sck(i) + [("cmask",)], writes=sck(i))
            S.op("dve", lambda e, o=o, w_i=w_i: e.tensor_tensor(out=sc[:, o + 2 * w_i - 128:o + 2 * w_i], in0=sc[:, o + 2 * w_i - 128:o + 2 * w_i],
                                                                in1=cmask[:, 128:256], op=ALU.add),
                 reads=sck(i) + [("cmask",)], writes=sck(i))
            if i >= 1:
                S.op("dve", lambda e, o=o, w_i=w_i, i=i: e.tensor_reduce(out=hi[:, i:i + 1], in_=sc[:, o:o + 2 * w_i], axis=AX.X, op=ALU.max),
                     reads=sck(i) + [("tk",)], writes=[("tk",)])
                S.op("dve", lambda e, o=o, w_i=w_i, i=i: e.tensor_reduce(out=t1[:, i:i + 1], in_=sc[:, o:o + w_i - 128], axis=AX.X, op=ALU.min),
                     reads=sck(i) + [("tk",)], writes=[("tk",)])
                S.op("dve", lambda e, o=o, w_i=w_i, i=i: e.tensor_reduce(out=t2[:, i:i + 1], in_=sc[:, o + w_i:o + 2 * w_i - 128], axis=AX.X, op=ALU.min),
                     reads=sck(i) + [("tk",)], writes=[("tk",)])
        S.op("dve", lambda e: e.tensor_tensor(out=lo[:, 1:8], in0=t1[:, 1:8], in1=t2[:, 1:8], op=ALU.min), reads=[("tk",)], writes=[("tk",)])
        S.op("dve", lambda e: e.tensor_tensor(out=w0[:, 1:8], in0=hi[:, 1:8], in1=lo[:, 1:8], op=ALU.subtract), reads=[("tk",)], writes=[("tk",)])
        q_next = 0
        for it in range(NIT):
            f = 2.0 ** (-(it + 1))
            S.op("dve", lambda e, f=f: e.scalar_tensor_tensor(out=mid, in0=w0, scalar=f, in1=lo, op0=ALU.mult, op1=ALU.add),
                 reads=[("tk",)], writes=[("tk",)])
            S.op("dve", lambda e: e.memset(cnt, 0.0), reads=[("tk",)], writes=[("tk",)])
            for i in range(1, 8):
                w_i = 128 * (i + 1)
                o = soff[i]
                S.op("dve", lambda e, o=o, w_i=w_i, i=i: e.tensor_scalar(out=junk[:, 0:2 * w_i], in0=sc[:, o:o + 2 * w_i], scalar1=mid[:, i:i + 1],
                                                                         scalar2=0.0, op0=ALU.is_ge, op1=ALU.add, accum_out=cnt[:, i:i + 1]),
                     reads=sck(i) + [("tk",)], writes=[("junk",), ("tk",)])
            S.op("dve", lambda e: e.scalar_tensor_tensor(out=pw, in0=cnt, scalar=255.5, in1=w0, op0=ALU.is_ge, op1=ALU.mult),
                 reads=[("tk",)], writes=[("tk",)])
            S.op("dve", lambda e, f=f: e.scalar_tensor_tensor(out=lo, in0=pw, scalar=f, in1=lo, op0=ALU.mult, op1=ALU.add),
                 reads=[("tk",)], writes=[("tk",)])
            if q_next < 16:
                q_head(q_next)
                q_next += 1
        while q_next < 16:
            q_head(q_next)
            q_next += 1
        flush_deferred()
        A.free("hT")
        sel = A.alloc("sel", BF16, [9216])
        for i in range(8):
            w_i = 128 * (i + 1)
            o = soff[i]
            S.op("dve", lambda e, o=o, w_i=w_i, i=i: e.tensor_scalar(out=sel[:, o:o + 2 * w_i], in0=sc[:, o:o + 2 * w_i], scalar1=lo[:, i:i + 1],
                                                                     scalar2=None, op0=ALU.is_ge),
                 reads=sck(i) + [("tk",)], writes=[("sel", i)])
        dbg_out("tk", tk, [128, 64], [("tk",)])
        ckpt("tk")
        S.phase = "selT_bias"
        selT = A.alloc("selT", BF16, [72, 128])
        toff = [i * (i + 1) for i in range(8)]
        for i in range(8):
            nt = 2 * (i + 1)
            for j0 in range(0, nt, 8):
                jn = min(8, nt - j0)
                bank = 4 + ring("xtp", 2)
                for jj in range(jn):
                    col = soff[i] + (j0 + jj) * 128
                    S.op("pe", lambda e, bank=bank, jj=jj, col=col: e.transpose(out=PSB(bank)[:, jj * 128:(jj + 1) * 128],
                                                                               in_=sel[:, col:col + 128], identity=identb),
                         reads=[("sel", i), ("identb",)], writes=[pk(bank)])
                t0 = toff[i] + j0
                S.op("dve", lambda e, bank=bank, jn=jn, t0=t0: e.tensor_copy(out=selT[:, t0:t0 + jn, :],
                                                                            in_=PSB(bank)[:, 0:jn * 128].rearrange("p (a b) -> p a b", b=128)),
                     reads=[pk(bank)], writes=[("selT", i, j0)])
        A.free("sc", "junk", "tk", "sel")

        EB = A.alloc("EB", BF16, [3, 16, 128])
        rbs = A.alloc("rbs", F32, [16]); ohs = A.alloc("ohs", F32, [383]); vecs = A.alloc("vecs", F32, [383])
        Hk = A.alloc("Hk", F32, [2, 16, 128])
        S.dma("sp", S.chan("ld_rbs"), rbs[0:32, :], rb_d, writes=[("rbs",)])
        S.dma("sp", S.chan("ld_ohs"), ohs[0:32, :], oh_d, writes=[("ohs",)])
        S.op("pe", lambda e: e.matmul(PS(0)[0:16, 0:383], lhsT=rbs[0:32, :], rhs=ohs[0:32, :], start=True, stop=True),
             reads=[("rbs",), ("ohs",)], writes=[pk(0)])
        S.op("dve", lambda e: e.tensor_copy(out=c31[0:16, :], in_=PS(0)[0:16, 0:1]), reads=[pk(0)], writes=[("c31",)])
        S.op("dve", lambda e: e.tensor_scalar(out=vecs[0:16, :], in0=PS(0)[0:16, 0:383], scalar1=c31[0:16, :],
                                              scalar2=None, op0=ALU.subtract), reads=[pk(0), ("c31",)], writes=[("vecs",)])
        scc = S.chan("scc")
        S.dma("sp", scc, scr_d, vecs[0:16, :], reads=[("vecs",)], writes=[("scr",)])
        for m in range(2):
            base = 0 if m == 1 else 128
            src = bass.AP(tensor=scr_t, offset=base, ap=[[1, 128], [383, 16], [1, 128]])
            S.dma("sp", S.chan("ld_hk%d" % m), Hk[:, m, :, :], src, reads=[("scr",)], writes=[("Hk", m)])
        for m in range(2):
            for hq in range(4):
                bank = 1 + (m * 4 + hq) % 2
                for hh in range(4):
                    h = hq * 4 + hh
                    S.op("pe", lambda e, bank=bank, m=m, h=h, hh=hh: e.matmul(
                        PS(bank)[:, hh * 128:(hh + 1) * 128], lhsT=Hk[:, m, h, :], rhs=Jf, start=True, stop=True),
                        reads=[("Hk", m), ("Jf",)], writes=[pk(bank)])
                dst = EB[:, 0 if m == 0 else 1, hq * 4:(hq + 1) * 4, :]
                S.op("act", lambda e, bank=bank, dst=dst: e.activation(
                    out=dst, in_=PS(bank).rearrange("p (a b) -> p a b", a=4), func=AF.Exp),
                    reads=[pk(bank)], writes=[("EB", m, hq)])
                if m == 1:
                    dst2 = EB[:, 2, hq * 4:(hq + 1) * 4, :]
                    S.op("act", lambda e, bank=bank, dst2=dst2: e.activation(
                        out=dst2, in_=PS(bank).rearrange("p (a b) -> p a b", a=4), func=AF.Exp, scale=flag0),
                        reads=[pk(bank), ("flag0",)], writes=[("EB", 2, hq)])
        A.free("rbs", "ohs", "vecs", "Hk")

        S.phase = "attn"
        oT = A.alloc("catB", BF16, [16, 1024], top=True)
        Eb = A.alloc("Eb", BF16, [5, 512]); Pb = A.alloc("Pb", BF16, [5, 512]); rden = A.alloc("rden", F32, [2, 512])
        def attention_group(g):
            items = []
            for i in range(8):
                tl = [(j, j, 0 if j == i else None) for j in range(i + 1)]
                tl += [(8 + j, (i + 1) + j, 1 if j == i else (2 if j == i - 1 else None)) for j in range(i + 1)]
                for n, (L, tj, kind) in enumerate(tl):
                    items.append((i, L, tj, kind, n == 0, n == len(tl) - 1))
            pend = []

            def do_pv(it):
                (i, L, tj, kind, first, last, pbuf) = it
                ob = 2 + (i % 2) * 2
                S.op("pe", lambda e: e.matmul(PS(ob), lhsT=Vsb[:, L, g * 128:(g + 1) * 128], rhs=Pb[:, pbuf, :], start=first, stop=last),
                     reads=[("Pb", pbuf), ("Vsb", g, L // 4)], writes=[pk(ob)])
                S.op("pe", lambda e: e.matmul(PS(ob + 1), lhsT=onesb, rhs=Pb[:, pbuf, :], start=first, stop=last),
                     reads=[("Pb", pbuf), ("onesb",)], writes=[pk(ob + 1)])
                if last:
                    rb = ring("rden", 2)
                    S.op("dve", lambda e: e.reciprocal(out=rden[:, rb, :], in_=PS(ob + 1)), reads=[pk(ob + 1)], writes=[("rden", rb)])
                    S.op("dve", lambda e: e.tensor_tensor(out=oT[:, g * 4:(g + 1) * 4, i * 128:(i + 1) * 128],
                                                          in0=PS(ob).rearrange("p (a b) -> p a b", b=128),
                                                          in1=rden[:, rb, :].rearrange("p (a b) -> p a b", b=128), op=ALU.mult),
                         reads=[pk(ob), ("rden", rb)], writes=[("catB", g, i)])

            for (i, L, tj, kind, first, last) in items:
                lb = (0, 1, 6, 7)[ring("lgps", 4)]
                S.op("pe", lambda e, lb=lb, L=L, i=i: e.matmul(PS(lb), lhsT=kT[:, g, L * 128:(L + 1) * 128],
                                                               rhs=qT[:, i, g * 4:(g + 1) * 4, :], start=True, stop=True),
                     reads=[("kT", g, L // 4)] + [("qT", g * 4 + r, i // 4) for r in range(4)], writes=[pk(lb)])
                eb = ring("Eb", 5)
                S.op("act", lambda e, lb=lb, eb=eb: e.activation(out=Eb[:, eb, :], in_=PS(lb), func=AF.Exp),
                     reads=[pk(lb)], writes=[("Eb", eb)])
                pbuf = ring("Pb", 5)
                mask = selT[:, toff[i] + tj, :]
                mask_b = mask.unsqueeze(1).broadcast_to([128, 4, 128])
                selk = ("selT", i, (tj // 8) * 8)
                S.op("dve", lambda e, eb=eb, pbuf=pbuf, mask_b=mask_b: e.tensor_tensor(
                    out=Pb[:, pbuf, :].rearrange("p (a b) -> p a b", b=128), in0=Eb[:, eb, :].rearrange("p (a b) -> p a b", b=128),
                    in1=mask_b, op=ALU.mult), reads=[("Eb", eb), selk], writes=[("Pb", pbuf)])
                if kind is not None:
                    ebt = EB[:, kind, g * 4:(g + 1) * 4, :]
                    S.op("dve", lambda e, pbuf=pbuf, ebt=ebt: e.tensor_tensor(
                        out=Pb[:, pbuf, :].rearrange("p (a b) -> p a b", b=128), in0=Pb[:, pbuf, :].rearrange("p (a b) -> p a b", b=128),
                        in1=ebt, op=ALU.mult), reads=[("Pb", pbuf), ("EB", min(kind, 1), g), ("EB", 2, g)], writes=[("Pb", pbuf)])
                pend.append((i, L, tj, kind, first, last, pbuf))
                if len(pend) > 3:
                    do_pv(pend.pop(0))
            while pend:
                do_pv(pend.pop(0))

        for g in range(4):
            attention_group(g)
        if dbg:
            dbg_out("qT", qT, [128, 8, 16, 128], [("qT", h, p) for h in range(16) for p in range(2)], dt=BF16)
            dbg_out("EB", EB, [128, 3, 16, 128], [("EB", m, hq) for m in range(3) for hq in range(4)], dt=BF16)
            dbg_out("selT", selT, [128, 72, 128], [("selT", i, j0) for i in range(8) for j0 in range(0, 2 * (i + 1), 8)], dt=BF16)
        A.free("kT", "Vsb", "selT", "qT", "EB", "Eb", "Pb", "rden", "wI")
        dbg_out("oT", oT, [128, 16, 1024], [("catB", g, i) for g in range(4) for i in range(8)], dt=BF16)
        ckpt("att")

        def sumsq_feat(src, skeys_fn, nchunks, banks, dstname):
            dst = A.alloc(dstname, F32, [1024])
            for c in range(nchunks):
                for p in range(2):
                    tb = ring("tmpb", 3)
                    S.op("dve", lambda e, c=c, p=p, tb=tb: e.tensor_tensor(out=tmpb[:, tb, :], in0=src[:, c, p * 512:(p + 1) * 512],
                                                                            in1=src[:, c, p * 512:(p + 1) * 512], op=ALU.mult),
                         reads=skeys_fn(c, p), writes=[("tmpb", tb)])
                    S.op("pe", lambda e, c=c, p=p, tb=tb: e.matmul(PS(banks[p]), lhsT=onesb, rhs=tmpb[:, tb, :], start=(c == 0), stop=(c == nchunks - 1)),
                         reads=[("tmpb", tb), ("onesb",)], writes=[pk(banks[p])])
            for p in range(2):
                S.op("act", lambda e, p=p: e.activation(out=dst[:, p * 512:(p + 1) * 512], in_=PS(banks[p]), func=AF.Sqrt,
                                                        scale=1.0 / (128 * nchunks), bias=epsc), reads=[pk(banks[p]), ("epsc",)], writes=[(dstname, p)])
                S.op("dve", lambda e, p=p: e.reciprocal(out=dst[:, p * 512:(p + 1) * 512], in_=dst[:, p * 512:(p + 1) * 512]),
                     reads=[(dstname, p)], writes=[(dstname, p)])
            return dst

        S.phase = "rmsB_xpass2"
        rstdB = sumsq_feat(oT, lambda c, p: [("catB", c // 4, i) for i in range(4 * p, 4 * p + 4)], 16, [2, 3], "rstdB")

        hT = A.alloc("hT2", BF16, [32, 1024])
        hT2 = hT

        def x_pass2(tok0, hTb):
            xt = A.alloc("xt", F32, [2, 4096]); xn = A.alloc("xn", BF16, [2, 4096])
            st1 = A.alloc("st1", F32, [32])
            S.op("dve", lambda e: e.memset(st1, 0.0), writes=[("st1", q) for q in range(24)])
            for t in range(8):
                b = ring("xt", 2)
                nb = ring("xn", 2)
                S.dma("sp", xch[b], xt[:, b, :], x_d[tok0 + t * 128: tok0 + (t + 1) * 128, :], writes=[("xt", b)])
                S.op("act", lambda e, b=b, t=t, nb=nb: e.activation(out=xn[:, nb, :], in_=xt[:, b, :], func=AF.Square, accum_out=st1[:, t:t + 1]),
                     reads=[("xt", b)], writes=[("xn", nb), ("st1", t)])
                S.op("act", lambda e, t=t: e.activation(out=st1[:, 8 + t:9 + t], in_=st1[:, t:t + 1], func=AF.Sqrt, scale=1.0 / 4096, bias=epsc),
                     reads=[("st1", t), ("epsc",)], writes=[("st1", 8 + t)])
                S.op("dve", lambda e, t=t: e.reciprocal(out=st1[:, 16 + t:17 + t], in_=st1[:, 8 + t:9 + t]),
                     reads=[("st1", 8 + t)], writes=[("st1", 16 + t)])
                S.op("act", lambda e, b=b, nb=nb, t=t: e.activation(out=xn[:, nb, :], in_=xt[:, b, :], func=AF.Copy, scale=st1[:, 16 + t:17 + t]),
                     reads=[("xt", b), ("st1", 16 + t)], writes=[("xn", nb)])
                for k8 in range(4):
                    bank = 4 + ring("xtp", 2)
                    for kk in range(8):
                        kc = k8 * 8 + kk
                        S.op("pe", lambda e, bank=bank, kk=kk, kc=kc, nb=nb: e.transpose(
                            out=PSB(bank)[:, kk * 128:(kk + 1) * 128], in_=xn[:, nb, kc * 128:(kc + 1) * 128], identity=identb),
                            reads=[("xn", nb), ("identb",)], writes=[pk(bank)])
                    for kk in range(8):
                        kc = k8 * 8 + kk
                        src = PSB(bank)[:, kk * 128:(kk + 1) * 128]
                        dst = hTb[:, kc, t * 128:(t + 1) * 128]
                        if bank == 4:
                            S.op("dve", lambda e, src=src, dst=dst, kc=kc: e.tensor_scalar(
                                out=dst, in0=src, scalar1=a_sc[:, kc:kc + 1], scalar2=shiftT[:, kc:kc + 1], op0=ALU.mult, op1=ALU.add),
                                reads=[pk(bank), ("a_sc",), ("modT",)], writes=[("hT2", kc, t)])
                        else:
                            S.op("act", lambda e, src=src, dst=dst, kc=kc: e.activation(
                                out=dst, in_=src, func=AF.Identity, scale=a_sc[:, kc:kc + 1], bias=shiftT[:, kc:kc + 1]),
                                reads=[pk(bank), ("a_sc",), ("modT",)], writes=[("hT2", kc, t)])
            A.free("xt", "xn", "st1")

        x_pass2(0, hT2)

        def project2(col0, ncols, pieces, consumer):
            b = load_w(win_v, col0, ncols)
            for p in pieces:
                bank = (0, 1, 7)[ring("projps", 3)]
                for kc in range(32):
                    S.op("pe", lambda e, b=b, kc=kc, p=p, bank=bank: e.matmul(
                        PS(bank)[0:ncols, :], lhsT=wt[:, b, kc, 0:ncols], rhs=hT2[:, kc, p * 512:(p + 1) * 512],
                        start=(kc == 0), stop=(kc == 31)),
                        reads=[("wt", b)] + [("hT2", kc, 4 * p + tt) for tt in range(4)], writes=[pk(bank)])
                consumer(p, bank)

        def gate_consumer(cat, cname, rstd, rname, gcol, gkey, c):
            def cons(p, bank):
                tb = ring("tmpb", 3)
                S.op("act", lambda e: e.activation(out=tmpb[:, tb, :], in_=PS(bank), func=AF.Silu), reads=[pk(bank)], writes=[("tmpb", tb)])
                tb2 = ring("tmpb", 3)
                ck = [(cname, c // 4, i) for i in range(4 * p, 4 * p + 4)] if cname == "catB" else [(cname, c, p)]
                S.op("dve", lambda e: e.tensor_tensor(out=tmpb[:, tb2, :], in0=cat[:, c, p * 512:(p + 1) * 512], in1=rstd[:, p * 512:(p + 1) * 512],
                                                      op=ALU.mult), reads=ck + [(rname, p)], writes=[("tmpb", tb2)])
                S.op("dve", lambda e: e.scalar_tensor_tensor(out=cat[:, c, p * 512:(p + 1) * 512], in0=tmpb[:, tb2, :], scalar=gcol,
                                                             in1=tmpb[:, tb, :], op0=ALU.mult, op1=ALU.mult),
                     reads=[("tmpb", tb2), ("tmpb", tb), gkey], writes=ck)
            return cons

        S.phase = "gB"
        for h in range(16):
            project2(C_GB + h * 128, 128, [0, 1], gate_consumer(oT, "catB", rstdB, "rstdB", onT[:, 16 + h:17 + h], ("onT",), h))
        A.free("rstdB")
        dbg_out("catB", oT, [128, 16, 1024], [("catB", g, i) for g in range(4) for i in range(8)], dt=BF16)

        S.phase = "A_v"
        catA = A.alloc("catA", BF16, [16, 1024])
        wsn = A.alloc("wsn", F32, [16, 128]); wsT = A.alloc("wsT", BF16, [16, 128]); bsb = A.alloc("bsb", F32, [16, 128])
        S.dma("sp", S.chan("ld_wsn"), wsn, ws_d.rearrange("c t s -> t c s"), writes=[("wsn",)])
        S.dma("sp", S.chan("ld_bsb"), bsb.rearrange("p a b -> p (a b)"), bs_d.partition_broadcast(128), writes=[("bsb",)])
        for c4 in range(4):
            bank = 4 + ring("xtp", 2)
            for cc in range(4):
                c = c4 * 4 + cc
                S.op("pe", lambda e, bank=bank, cc=cc, c=c: e.transpose(out=PS(bank)[:, cc * 128:(cc + 1) * 128], in_=wsn[:, c, :], identity=identf),
                     reads=[("wsn",), ("identf",)], writes=[pk(bank)])
            S.op("dve", lambda e, bank=bank, c4=c4: e.tensor_tensor(out=wsT[:, c4 * 4:(c4 + 1) * 4, :], in0=PS(bank).rearrange("p (a b) -> p a b", b=128),
                                                                    in1=trilT.unsqueeze(1).broadcast_to([128, 4, 128]), op=ALU.mult),
                 reads=[pk(bank), ("trilT",)], writes=[("wsT", c4)])
        A.free("wsn")
        GELU = AF.Gelu_apprx_tanh
        vdef = []
        GELU_COMP = os.environ.get("K_GELU", "lut") == "comp"

        def gelu_to(dst, dkey, bank):
            if not GELU_COMP:
                S.op("act", lambda e: e.activation(out=dst, in_=PS(bank), func=GELU), reads=[pk(bank)], writes=[dkey])
                return
            tf = ring("tmpf", 2)
            S.op("act", lambda e: e.activation(out=tmpf[:, tf, :], in_=PS(bank), func=AF.Square), reads=[pk(bank)], writes=[("tmpf", tf)])
            S.op("dve", lambda e: e.tensor_scalar(out=tmpf[:, tf, :], in0=tmpf[:, tf, :], scalar1=0.044715, scalar2=1.0, op0=ALU.mult, op1=ALU.add),
                 reads=[("tmpf", tf)], writes=[("tmpf", tf)])
            S.op("dve", lambda e: e.tensor_tensor(out=tmpf[:, tf, :], in0=tmpf[:, tf, :], in1=PS(bank), op=ALU.mult),
                 reads=[("tmpf", tf), pk(bank)], writes=[("tmpf", tf)])
            S.op("act", lambda e: e.activation(out=tmpf[:, tf, :], in_=tmpf[:, tf, :], func=AF.Sigmoid, scale=1.5957691216057308),
                 reads=[("tmpf", tf)], writes=[("tmpf", tf)])
            S.op("dve", lambda e: e.tensor_tensor(out=dst, in0=tmpf[:, tf, :], in1=PS(bank), op=ALU.mult),
                 reads=[("tmpf", tf), pk(bank)], writes=[dkey])
        for c in range(16):
            def c_v(p, bank, c=c):
                gelu_to(catA[:, c, p * 512:(p + 1) * 512], ("catA", c, p), bank)
                tb = ring("tmpb", 3)
                S.op("dve", lambda e: e.tensor_tensor(out=tmpb[:, tb, :], in0=catA[:, c, p * 512:(p + 1) * 512], in1=catA[:, c, p * 512:(p + 1) * 512],
                                                       op=ALU.mult), reads=[("catA", c, p)], writes=[("tmpb", tb)])
                while vdef:
                    vdef.pop(0)()

                def stat_mm(c=c, p=p, tb=tb):
                    S.op("pe", lambda e: e.matmul(PS(2 + p), lhsT=onesb, rhs=catA[:, c, p * 512:(p + 1) * 512], start=(c == 0), stop=(c == 15)),
                         reads=[("catA", c, p), ("onesb",)], writes=[pk(2 + p)])
                    S.op("pe", lambda e: e.matmul(PS(4 + p), lhsT=onesb, rhs=tmpb[:, tb, :], start=(c == 0), stop=(c == 15)),
                         reads=[("tmpb", tb), ("onesb",)], writes=[pk(4 + p)])
                vdef.append(stat_mm)
            project2(C_V + c * 128, 128, [0, 1], c_v)
        while vdef:
            vdef.pop(0)()
        S.phase = "A_main"
        meanv = A.alloc("meanv", F32, [1024]); rstdv = A.alloc("rstdv", F32, [1024])
        for p in range(2):
            sl = slice(p * 512, (p + 1) * 512)
            S.op("dve", lambda e, p=p, sl=sl: e.tensor_scalar(out=meanv[:, sl], in0=PS(2 + p), scalar1=1.0 / 2048, scalar2=None, op0=ALU.mult),
                 reads=[pk(2 + p)], writes=[("meanv", p)])
            S.op("dve", lambda e, p=p, sl=sl: e.tensor_scalar(out=rstdv[:, sl], in0=PS(4 + p), scalar1=1.0 / 2048, scalar2=EPS, op0=ALU.mult, op1=ALU.add),
                 reads=[pk(4 + p)], writes=[("rstdv", p)])
            tf = ring("tmpf", 2)
            S.op("dve", lambda e, sl=sl, tf=tf: e.tensor_tensor(out=tmpf[:, tf, :], in0=meanv[:, sl], in1=meanv[:, sl], op=ALU.mult),
                 reads=[("meanv", p)], writes=[("tmpf", tf)])
            S.op("dve", lambda e, sl=sl, tf=tf: e.tensor_tensor(out=rstdv[:, sl], in0=rstdv[:, sl], in1=tmpf[:, tf, :], op=ALU.subtract),
                 reads=[("rstdv", p), ("tmpf", tf)], writes=[("rstdv", p)])
            S.op("act", lambda e, sl=sl: e.activation(out=rstdv[:, sl], in_=rstdv[:, sl], func=AF.Sqrt),
                 reads=[("rstdv", p)], writes=[("rstdv", p)])
            S.op("dve", lambda e, sl=sl: e.reciprocal(out=rstdv[:, sl], in_=rstdv[:, sl]), reads=[("rstdv", p)], writes=[("rstdv", p)])
        vl = A.alloc("vl", F32, [2, 512]); vlnT = A.alloc("vlnT", BF16, [2, 1024]); vtok = A.alloc("vtok", BF16, [2, 8, 128])
        guT = A.alloc("guT", BF16, [2, 1024])
        def ln_dve(c):
            vb = c % 2
            for p in range(2):
                sl = slice(p * 512, (p + 1) * 512)
                lb = ring("vl", 2)
                S.op("dve", lambda e, sl=sl, lb=lb: e.tensor_tensor(out=vl[:, lb, :], in0=catA[:, c, sl], in1=meanv[:, sl], op=ALU.subtract),
                     reads=[("catA", c, p), ("meanv", p)], writes=[("vl", lb)])
                S.op("dve", lambda e, sl=sl, lb=lb: e.tensor_tensor(out=vl[:, lb, :], in0=vl[:, lb, :], in1=rstdv[:, sl], op=ALU.mult),
                     reads=[("vl", lb), ("rstdv", p)], writes=[("vl", lb)])
                S.op("dve", lambda e, sl=sl, lb=lb: e.tensor_scalar(out=vlnT[:, vb, sl], in0=vl[:, lb, :], scalar1=lnvT[:, c:c + 1],
                                                                    scalar2=lnvT[:, 16 + c:17 + c], op0=ALU.mult, op1=ALU.add),
                     reads=[("vl", lb), ("lnvT",)], writes=[("vlnT", vb, p)])

        def tr_pe(c):
            vb = c % 2
            for n in range(8):
                S.op("pe", lambda e, n=n: e.transpose(out=PSB(6)[:, n * 128:(n + 1) * 128], in_=vlnT[:, vb, n * 128:(n + 1) * 128], identity=identb),
                     reads=[("vlnT", vb, n // 4), ("identb",)], writes=[pk(6)])
            S.op("act", lambda e: e.activation(out=vtok[:, vb, :, :], in_=PSB(6).rearrange("p (a b) -> p a b", b=128), func=AF.Copy),
                 reads=[pk(6)], writes=[("vtok", vb)])

        def uproj(c):
            ub = c % 2

            def c_u(p, bank):
                gelu_to(guT[:, ub, p * 512:(p + 1) * 512], ("guT", ub, p), bank)
            project2(C_U + c * 128, 128, [0, 1], c_u)

        def tril(c):
            vb = c % 2
            ub = c % 2
            for p in range(2):
                for n4 in range(4):
                    n = p * 4 + n4
                    S.op("pe", lambda e, n=n, n4=n4, p=p: e.matmul(PS(2 + p)[:, n4 * 128:(n4 + 1) * 128], lhsT=vtok[:, vb, n, :], rhs=wsT[:, c, :],
                                                                   start=True, stop=True),
                         reads=[("vtok", vb), ("wsT", c // 4)], writes=[pk(2 + p)])
                tf = ring("tmpf", 2)
                S.op("dve", lambda e, p=p, tf=tf: e.tensor_tensor(out=tmpf[:, tf, :].rearrange("p (a b) -> p a b", b=128),
                                                                  in0=PS(2 + p).rearrange("p (a b) -> p a b", b=128),
                                                                  in1=bsb[:, c, :].unsqueeze(1).broadcast_to([128, 4, 128]), op=ALU.add),
                     reads=[pk(2 + p), ("bsb",)], writes=[("tmpf", tf)])
                S.op("dve", lambda e, p=p, tf=tf: e.tensor_tensor(out=catA[:, c, p * 512:(p + 1) * 512], in0=tmpf[:, tf, :],
                                                                  in1=guT[:, ub, p * 512:(p + 1) * 512], op=ALU.mult),
                     reads=[("tmpf", tf), ("guT", ub, p)], writes=[("catA", c, p)])

        ln_dve(0)
        tr_pe(0)
        for c in range(16):
            if c + 1 < 16:
                ln_dve(c + 1)
            uproj(c)
            if c + 1 < 16:
                tr_pe(c + 1)
            tril(c)
        A.free("meanv", "rstdv", "vl", "vlnT", "vtok", "guT", "wsT", "bsb")
        dbg_out("yA", catA, [128, 16, 1024], [("catA", c, p) for c in range(16) for p in range(2)], dt=BF16)
        ckpt("yA")
        S.phase = "gA"
        rstdA = sumsq_feat(catA, lambda c, p: [("catA", c, p)], 16, [4, 5], "rstdA")
        for c in range(16):
            project2(C_GA + c * 128, 128, [0, 1], gate_consumer(catA, "catA", rstdA, "rstdA", onT[:, c:c + 1], ("onT",), c))
        A.free("rstdA", "hT2", "wt", "tmpb", "tmpf")
        dbg_out("catA", catA, [128, 16, 1024], [("catA", c, p) for c in range(16) for p in range(2)], dt=BF16)

        S.phase = "outproj"
        gbc = A.alloc("gbc", F32, [4096]); dg = A.alloc("dg", F32, [2, 128])
        for kc in range(32):
            db = ring("dg", 2)
            S.op("dve", lambda e, db=db, kc=kc: e.tensor_scalar(out=dg[:, db, :], in0=identf, scalar1=gateT[:, kc:kc + 1], scalar2=None, op0=ALU.mult),
                 reads=[("identf",), ("modT",)], writes=[("dg", db)])
            bank = 4 + (kc // 4) % 2
            S.op("pe", lambda e, db=db, kc=kc, bank=bank: e.matmul(PS(bank)[:, (kc % 4) * 128:(kc % 4 + 1) * 128], lhsT=onesf, rhs=dg[:, db, :], start=True, stop=True),
                 reads=[("dg", db), ("onesf",)], writes=[pk(bank)])
            if kc % 4 == 3:
                S.op("act", lambda e, kc=kc, bank=bank: e.activation(out=gbc[:, (kc - 3) * 128:(kc + 1) * 128], in_=PS(bank), func=AF.Copy),
                     reads=[pk(bank)], writes=[("gbc", kc // 4)])
        wo = A.alloc("wo", BF16, [2, 32, 512])
        woch = [S.chan(f"woch{i}") for i in range(2)]
        xr = A.alloc("xr", F32, [3, 512]); yo = A.alloc("yo", F32, [3, 512])
        xrch = [S.chan(f"xrch{i}") for i in range(3)]
        och = [S.chan(f"och{i}") for i in range(3)]
        wout_v = wout_d.rearrange("(kc p) n -> p kc n", p=128)
        outs = []
        for n in range(8):
            wb = ring("wo", 2)
            for half in range(2):
                S.dma("pool", woch[wb], wo[:, wb, half * 16:(half + 1) * 16, :], wout_v[:, half * 16:(half + 1) * 16, n * 512:(n + 1) * 512],
                      writes=[("wo", wb, half)])
            for t in range(8):
                bank = (0, 1, 7)[ring("projps", 3)]
                for kc in range(32):
                    src = catA[:, kc, t * 128:(t + 1) * 128] if kc < 16 else oT[:, kc - 16, t * 128:(t + 1) * 128]
                    rk = ("catA", kc, t // 4) if kc < 16 else ("catB", (kc - 16) // 4, t)
                    S.op("pe", lambda e, src=src, wb=wb, kc=kc, bank=bank: e.matmul(PS(bank), lhsT=src, rhs=wo[:, wb, kc, :], start=(kc == 0), stop=(kc == 31)),
                         reads=[rk, ("wo", wb, 0), ("wo", wb, 1)], writes=[pk(bank)])
                xb = ring("xr", 3)
                S.dma("sp", xrch[xb], xr[:, xb, :], x_d[t * 128:(t + 1) * 128, n * 512:(n + 1) * 512], writes=[("xr", xb)])
                yb = ring("yo", 3)
                S.op("dve", lambda e, bank=bank, yb=yb, n=n: e.tensor_tensor(out=yo[:, yb, :], in0=PS(bank), in1=gbc[:, n * 512:(n + 1) * 512], op=ALU.mult),
                     reads=[pk(bank), ("gbc", n)], writes=[("yo", yb)])
                S.op("dve", lambda e, yb=yb, xb=xb: e.tensor_tensor(out=yo[:, yb, :], in0=yo[:, yb, :], in1=xr[:, xb, :], op=ALU.add),
                     reads=[("yo", yb), ("xr", xb)], writes=[("yo", yb)])
                outs.append(S.dma("sp", och[yb], out_d[t * 128:(t + 1) * 128, n * 512:(n + 1) * 512], yo[:, yb, :], reads=[("yo", yb)]))
        S.emit(final_waits=outs[-3:] + list(dbg_d.values()))


def _t5_bucket(d):
    n = max(d, 0)
    if n < 16:
        return n
    nf = np.float32(max(n, 1))
    v = np.log(nf / np.float32(16)) / np.float32(math.log(128 / 16)) * np.float32(16)
    return min(16 + int(np.float32(v)), 31)


def _consts(j):
    ident = np.eye(128, dtype=np.float32)
    Jm = np.ascontiguousarray(ident[::-1])
    s = np.arange(128)[:, None]; t = np.arange(128)[None, :]
    trilT = (t >= s).astype(np.float32)
    cm = np.zeros((128, 256), np.float32)
    cm[:, 0:128] = np.where(np.arange(128)[None, :] <= np.arange(128)[:, None], 0.0, NEG)
    cm[:, 128:256] = 0.0 if j == 1 else NEG
    oh = np.zeros((32, 383), np.float32)
    for i in range(383):
        oh[_t5_bucket(255 - i), i] = 1.0
    flag = np.full((128, 1), 1.0 if j == 0 else 0.0, np.float32)
    return {"ident": ident, "Jm": Jm, "trilT": trilT, "cmask": cm, "ohrev": oh, "flag0": flag}


def _colT(v, n):
    return np.ascontiguousarray(v.reshape(n, 128).T)


_NC_CACHE = {}


def kernel(x, c, w_ada, b_ada, norm_g, w_in, ln_v_g, ln_v_b, w_s, b_s, q_norm_g, k_norm_g, rel_bias,
           out_norm_a_g, out_norm_b_g, w_out, _dbg=False):
    x = np.asarray(x, np.float32)
    f = lambda a: np.ascontiguousarray(np.asarray(a, np.float32))
    w_ada0, w_in0, w_out0 = f(w_ada[0]), f(w_in[0]), f(w_out[0])
    shared = {
        "w_ada": w_ada0, "w_in": w_in0, "w_out": w_out0,
        "b_adaT": _colT(f(b_ada[0]), 96), "norm_gT": _colT(f(norm_g[0]), 32),
        "lnvT": np.ascontiguousarray(np.concatenate([_colT(f(ln_v_g[0]), 16), _colT(f(ln_v_b[0]), 16)], axis=1)),
        "w_s": f(w_s[0]), "b_s": f(b_s[0]).reshape(1, 2048),
        "qk_g": np.ascontiguousarray(np.stack([f(q_norm_g[0]), f(k_norm_g[0])], axis=1)),
        "rel_bias": f(rel_bias),
        "onT": np.ascontiguousarray(np.concatenate([_colT(f(out_norm_a_g[0]), 16), _colT(f(out_norm_b_g[0]), 16)], axis=1)),
    }
    in_maps = []
    for core in range(8):
        b, j = core // 2, core % 2
        xb = x[b].reshape(16, 128, 4096)
        own = xb[j::2]; oth = xb[1 - j::2]
        xl = np.ascontiguousarray(np.concatenate([own, oth], axis=0).reshape(2048, 4096))
        m = dict(shared)
        m.update(_consts(j))
        m["x"] = xl
        m["cT"] = _colT(f(c[b]), 32)
        in_maps.append(m)
    key = bool(_dbg)
    if key not in _NC_CACHE:
        _NC_CACHE[key] = build_nc(dbg=key)
    nc = _NC_CACHE[key]
    ncores = int(os.environ.get("K_CORES", "8"))
    res = run_bass_kernel_spmd(nc, in_maps[:ncores], core_ids=list(range(ncores)))
    out = np.zeros((4, 16, 128, 4096), np.float32)
    for core in range(ncores):
        b, j = core // 2, core % 2
        out[b, j::2] = np.asarray(res.results[core]["out"], np.float32).reshape(8, 128, 4096)
    if _dbg:
        return out.reshape(4, 2048, 4096), res
    return out.reshape(4, 2048, 4096)
```

```python
import os
import math
import numpy as np
import ml_dtypes
import concourse.bass as bass
import concourse.mybir as mybir
from concourse.bass_utils import run_bass_kernel_spmd
from contextlib import ExitStack

F32 = mybir.dt.float32
BF16 = mybir.dt.bfloat16
ALU = mybir.AluOpType
AF = mybir.ActivationFunctionType
AX = mybir.AxisListType

EPS = 1e-6
NIT = 22
NEG = -1.0e30
ARENA_F32 = int(os.environ.get("K_ARENA", "53200"))
C_U, C_V, C_GA, C_Q, C_K, C_VB, C_GB, C_QI, C_KI, C_WI = 0, 2048, 4096, 6144, 8192, 8704, 9216, 11264, 15360, 15488


class _Op:
    __slots__ = ("eng", "fn", "deps", "sem", "inc", "val", "needed", "idx", "phase")

    def __init__(self, eng, fn, deps, sem, inc, idx):
        self.eng, self.fn, self.deps, self.sem, self.inc, self.idx = eng, fn, deps, sem, inc, idx
        self.val = None
        self.needed = False


class Sched:
    ENGS = ("pe", "act", "dve", "pool", "sp")

    def __init__(self, nc):
        self.nc = nc
        self.ops = {e: [] for e in self.ENGS}
        self.last_w = {}
        self.readers = {}
        self.ghost = {}
        self.known = set()
        self.chan_names = []
        self.n = 0
        self.phase = "setup"
        self.scopes = os.environ.get("K_SCOPES", "") == "1"

    def chan(self, name):
        self.chan_names.append(name)
        return name

    def retire(self, old, new):
        g = self.ghost.setdefault(new, {})
        for k, w in self.last_w.items():
            if k[0] == old:
                self._merge(g, w)
        for k, r in self.readers.items():
            if k[0] == old:
                for o in r.values():
                    self._merge(g, o)
        for o in self.ghost.get(old, {}).values():
            self._merge(g, o)

    def forget(self, name):
        g = {}
        for k in [k for k in self.last_w if k[0] == name]:
            self._merge(g, self.last_w.pop(k))
        for k in [k for k in self.readers if k[0] == name]:
            for o in self.readers.pop(k).values():
                self._merge(g, o)
        for o in self.ghost.get(name, {}).values():
            self._merge(g, o)
        self.ghost[name] = g
        self.known = {k for k in self.known if k[0] != name}

    @staticmethod
    def _merge(d, op):
        cur = d.get(op.sem)
        if cur is None or cur.idx < op.idx:
            d[op.sem] = op

    def _add(self, eng, fn, reads, writes, sem, inc):
        deps = {}
        for k in list(reads) + list(writes):
            if k not in self.known:
                self.known.add(k)
                for o in self.ghost.get(k[0], {}).values():
                    self._merge(deps, o)
            w = self.last_w.get(k)
            if w is not None:
                self._merge(deps, w)
        for k in writes:
            for o in self.readers.get(k, {}).values():
                self._merge(deps, o)
        self.n += 1
        op = _Op(eng, fn, list(deps.values()), sem, inc, self.n)
        op.phase = self.phase
        for k in reads:
            self._merge(self.readers.setdefault(k, {}), op)
        for k in writes:
            self.last_w[k] = op
            self.readers[k] = {}
        self.ops[eng].append(op)
        return op

    def op(self, eng, fn, reads=(), writes=()):
        return self._add(eng, fn, reads, writes, eng, 1)

    def dma(self, q, chan, out, in_, reads=(), writes=()):
        return self._add(q, lambda e: e.dma_start(out=out, in_=in_), reads, writes, chan, 16)

    def emit(self, final_waits=()):
        nc = self.nc
        for e in self.ENGS:
            for op in self.ops[e]:
                for d in op.deps:
                    if d.eng == "pe" and op.eng == "pe" and d.sem == "pe":
                        continue
                    d.needed = True
        for op in final_waits:
            op.needed = True
        for e in self.ENGS:
            for op in self.ops[e]:
                if op.inc == 16:
                    op.needed = True
        counters = {}
        for e in self.ENGS:
            for op in self.ops[e]:
                if op.needed:
                    counters[op.sem] = counters.get(op.sem, 0) + op.inc
                    op.val = counters[op.sem]
        sem_keys = list(self.ENGS) + self.chan_names
        with ExitStack() as st:
            sems = {k: st.enter_context(nc.semaphore("s_" + k)) for k in sem_keys}
            block = st.enter_context(nc.Block())
            engmap = {"pe": block.tensor, "act": block.scalar, "dve": block.vector,
                      "pool": block.gpsimd, "sp": block.sync}

            def make(ename):
                def body(eng):
                    seen = {}
                    cur = [None, None]
                    for op in self.ops[ename]:
                        if self.scopes and op.phase != cur[0]:
                            if cur[1] is not None:
                                cur[1].__exit__(None, None, None)
                            cur[1] = nc.named_scope(op.phase)
                            cur[1].__enter__()
                            cur[0] = op.phase
                        need = {}
                        for d in op.deps:
                            if d.eng == "pe" and ename == "pe" and d.sem == "pe":
                                continue
                            if need.get(d.sem, 0) < d.val:
                                need[d.sem] = d.val
                        for k, v in need.items():
                            if seen.get(k, 0) < v:
                                eng.wait_ge(sems[k], v)
                                seen[k] = v
                        ins = op.fn(eng)
                        if op.needed:
                            ins.then_inc(sems[op.sem], op.inc)
                    if cur[1] is not None:
                        cur[1].__exit__(None, None, None)
                    if ename == "sp":
                        for op in final_waits:
                            if seen.get(op.sem, 0) < op.val:
                                eng.wait_ge(sems[op.sem], op.val)
                                seen[op.sem] = op.val
                return body

            for ename in self.ENGS:
                engmap[ename](make(ename))


class StopBuild(Exception):
    pass


class Arena:
    def __init__(self, nc, S, st, nfloats):
        self.S = S
        self.t = st.enter_context(nc.sbuf_tensor("arena", [128, nfloats], F32))
        self.n = nfloats
        self.live = {}
        self.dead = []

    def alloc(self, name, dtype, shape, top=False):
        nel = int(np.prod(shape))
        nf = nel if dtype == F32 else (nel + 1) // 2
        nf = (nf + 15) // 16 * 16
        spans = sorted((o, o + n) for (o, n) in self.live.values())
        if top:
            pos = self.n - nf
            for (a, b) in reversed(spans):
                if b <= pos:
                    break
                pos = min(pos, a - nf)
            assert pos >= 0
        else:
            pos = 0
            for (a, b) in spans:
                if a - pos >= nf:
                    break
                pos = max(pos, b)
        assert pos + nf <= self.n, f"SBUF arena overflow allocating {name} ({nf} floats), live={self.live}"
        self.live[name] = (pos, nf)
        self.S.forget(name)
        for (dn, do, dnf) in self.dead:
            if do < pos + nf and pos < do + dnf:
                self.S.retire(dn, name)
        ap = self.t[:, pos:pos + nf]
        if dtype != F32:
            ap = ap.bitcast(dtype)
        ap = ap[:, 0:nel]
        if len(shape) == 2:
            ap = ap.rearrange("p (a b) -> p a b", a=shape[0], b=shape[1])
        elif len(shape) == 3:
            ap = ap.rearrange("p (a b c) -> p a b c", a=shape[0], b=shape[1], c=shape[2])
        return ap

    def free(self, *names):
        for name in names:
            o, n = self.live.pop(name)
            self.dead.append((name, o, n))


def build_nc(dbg=False):
    nc = bass.Bass("TRN2", target_bir_lowering=False)

    def D(name, shape, kind="ExternalInput", dt=F32):
        return nc.dram_tensor(name, shape, dt, kind=kind).ap()

    x_d = D("x", [2048, 4096]); cT_d = D("cT", [128, 32]); wada_d = D("w_ada", [4096, 12288])
    badaT_d = D("b_adaT", [128, 96]); ngT_d = D("norm_gT", [128, 32]); win_d = D("w_in", [4096, 15520])
    lnv_d = D("lnvT", [128, 32]); ws_d = D("w_s", [16, 128, 128]); bs_d = D("b_s", [1, 2048])
    qkg_d = D("qk_g", [128, 2]); rb_d = D("rel_bias", [32, 16]); on_d = D("onT", [128, 32])
    wout_d = D("w_out", [4096, 4096])
    ident_d = D("ident", [128, 128]); trilT_d = D("trilT", [128, 128]); cmask_d = D("cmask", [128, 256])
    J_d = D("Jm", [128, 128]); oh_d = D("ohrev", [32, 383]); flag_d = D("flag0", [128, 1])
    out_d = D("out", [1024, 4096], kind="ExternalOutput")
    scr_t = nc.dram_tensor("scr", [16, 383], F32)
    scr_d = scr_t.ap()
    dbg_d = {}

    STOP = os.environ.get("K_STOP", "")

    def ckpt(name):
        if STOP == name:
            raise StopBuild()

    with ExitStack() as st:
        S = Sched(nc)
        A = Arena(nc, S, st, ARENA_F32)
        try:
            _build_body(nc, S, A, st, D, dbg, dbg_d, ckpt, x_d, cT_d, wada_d, badaT_d, ngT_d, win_d, lnv_d, ws_d, bs_d, qkg_d, rb_d,
                        on_d, wout_d, ident_d, trilT_d, cmask_d, J_d, oh_d, flag_d, out_d, scr_t, scr_d)
        except StopBuild:
            S.emit(final_waits=list(dbg_d.values()))
    return nc


def _build_body(nc, S, A, st, D, dbg, dbg_d, ckpt, x_d, cT_d, wada_d, badaT_d, ngT_d, win_d, lnv_d, ws_d, bs_d, qkg_d, rb_d,
                on_d, wout_d, ident_d, trilT_d, cmask_d, J_d, oh_d, flag_d, out_d, scr_t, scr_d):
    if True:
        pb = [st.enter_context(nc.psum_tensor(f"pb{i}", [128, 512], F32)) for i in range(8)]

        def PS(i):
            return pb[i][:]

        def PSB(i):
            return pb[i][:].bitcast(BF16)

        def pk(i):
            return ("ps%d" % i,)

        _ring = {}

        def ring(name, n):
            i = _ring.get(name, 0)
            _ring[name] = i + 1
            return i % n

        def dbg_out(name, ap_sb, shape, reads, dt=F32):
            if not dbg:
                return None
            d = D("dbg_" + name, shape, kind="ExternalOutput", dt=dt)
            dbg_d[name] = S.dma("sp", S.chan("dbg_" + name), d, ap_sb, reads=reads)
            return d

        identf = A.alloc("identf", F32, [128]); identb = A.alloc("identb", BF16, [128])
        onesb = A.alloc("onesb", BF16, [128]); onesf = A.alloc("onesf", F32, [128])
        Jf = A.alloc("Jf", F32, [128]); trilT = A.alloc("trilT", F32, [128]); cmask = A.alloc("cmask", F32, [256])
        cT = A.alloc("cT", F32, [32]); badaT = A.alloc("badaT", F32, [96]); ngT = A.alloc("ngT", F32, [32])
        lnvT = A.alloc("lnvT", F32, [32]); onT = A.alloc("onT", F32, [32]); qkg = A.alloc("qkg", F32, [16])[:, 0:2]
        flag0 = A.alloc("flag0", F32, [16])[:, 0:1]
        sme = A.alloc("sme", F32, [256])
        gqs = sme[:, 0:1]; c31 = sme[:, 1:2]; epsc = sme[:, 2:3]; epsc128 = sme[:, 3:4]
        modT = sme[:, 16:112]; a_sc = sme[:, 112:144]
        S.op("dve", lambda e: e.memset(epsc, EPS), writes=[("epsc",)])
        S.op("dve", lambda e: e.memset(epsc128, 128.0 * EPS), reads=[("epsc",)], writes=[("epsc",)])
        ldc = S.chan("ldc")
        small_loads = [(identf, ident_d, "identf"), (Jf, J_d, "Jf"), (trilT, trilT_d, "trilT"), (cmask, cmask_d, "cmask"),
                       (cT, cT_d, "cT"), (badaT, badaT_d, "badaT"), (ngT, ngT_d, "ngT"), (lnvT, lnv_d, "lnvT"),
                       (onT, on_d, "onT"), (qkg, qkg_d, "qkg"), (flag0, flag_d, "flag0")]
        for (dst, src, nm) in small_loads:
            S.dma("sp", S.chan("ld_" + nm), dst, src, writes=[(nm,)])
        S.op("dve", lambda e: e.tensor_copy(out=identb, in_=identf), reads=[("identf",)], writes=[("identb",)])
        S.op("dve", lambda e: e.memset(onesb, 1.0), writes=[("onesb",)])
        S.op("dve", lambda e: e.memset(onesf, 1.0), writes=[("onesf",)])
        S.op("dve", lambda e: e.tensor_scalar(out=gqs, in0=qkg[:, 0:1], scalar1=math.sqrt(128.0), scalar2=None,
                                              op0=ALU.mult), reads=[("qkg",)], writes=[("gqs",)])

        S.phase = "adaln"
        scb = A.alloc("scb", BF16, [32])
        S.op("act", lambda e: e.activation(out=scb, in_=cT, func=AF.Silu), reads=[("cT",)], writes=[("scb",)])
        wt = A.alloc("wt", BF16, [3, 32, 128])
        wch = [S.chan(f"wch{i}") for i in range(3)]
        wada_v = wada_d.rearrange("(kc p) n -> p kc n", p=128)
        win_v = win_d.rearrange("(kc p) n -> p kc n", p=128)

        def load_w(view, col0, ncols):
            b = ring("wt", 3)
            S.dma("pool", wch[b], wt[:, b, :, 0:ncols], view[:, :, col0:col0 + ncols], writes=[("wt", b)])
            return b

        for fcg in range(12):
            for kcg in range(8):
                b = ring("wt", 3)
                tile_ap = wt[:, b, :, :].rearrange("p a b -> p (a b)").rearrange("p (k n) -> p k n", k=4)
                S.dma("pool", wch[b], tile_ap, wada_v[:, kcg * 4:(kcg + 1) * 4, fcg * 1024:(fcg + 1) * 1024], writes=[("wt", b)])
                for f8 in range(8):
                    fc = fcg * 8 + f8
                    bank = 3 if fc < 64 else 4
                    col = (fc % 64) * 8 + kcg
                    for kk in range(4):
                        kc = kcg * 4 + kk
                        S.op("pe", lambda e, tile_ap=tile_ap, kk=kk, f8=f8, kc=kc, bank=bank, col=col: e.matmul(
                            PS(bank)[:, col:col + 1], lhsT=tile_ap[:, kk, f8 * 128:(f8 + 1) * 128], rhs=scb[:, kc:kc + 1],
                            start=(kk == 0), stop=(kk == 3)),
                            reads=[("wt", b), ("scb",)], writes=[pk(bank)])
        S.op("dve", lambda e: e.tensor_reduce(out=modT[:, 0:64], in_=PS(3).rearrange("p (f k) -> p f k", k=8), axis=AX.X, op=ALU.add),
             reads=[pk(3)], writes=[("modT",)])
        S.op("dve", lambda e: e.tensor_reduce(out=modT[:, 64:96], in_=PS(4)[:, 0:256].rearrange("p (f k) -> p f k", k=8), axis=AX.X, op=ALU.add),
             reads=[pk(4), ("modT",)], writes=[("modT",)])
        S.op("dve", lambda e: e.tensor_tensor(out=modT, in0=modT, in1=badaT, op=ALU.add),
             reads=[("modT",), ("badaT",)], writes=[("modT",)])
        shiftT = modT[:, 0:32]; scaleT = modT[:, 32:64]; gateT = modT[:, 64:96]
        S.op("dve", lambda e: e.scalar_tensor_tensor(out=a_sc, in0=scaleT, scalar=1.0, in1=ngT, op0=ALU.add, op1=ALU.mult),
             reads=[("modT",), ("ngT",)], writes=[("a_sc",)])
        A.free("scb")
        dbg_out("modT", modT, [128, 96], [("modT",)])
        ckpt("ada")

        xch = [S.chan(f"xch{i}") for i in range(2)]

        XPL = int(os.environ.get("K_XP", "9"))
        def x_pass(tok0, hT):
            xt = A.alloc("xt", F32, [2, 4096]); xn = A.alloc("xn", BF16, [2, 4096])
            st1 = A.alloc("st1", F32, [32])
            S.op("dve", lambda e: e.memset(st1, 0.0), writes=[("st1", q) for q in range(24)])
            for t in range(8):
                b = ring("xt", 2)
                nb = ring("xn", 2)
                S.dma("sp", xch[b], xt[:, b, :], x_d[tok0 + t * 128: tok0 + (t + 1) * 128, :], writes=[("xt", b)])
                S.op("act", lambda e, b=b, t=t, nb=nb: e.activation(out=xn[:, nb, :], in_=xt[:, b, :], func=AF.Square,
                                                                    accum_out=st1[:, t:t + 1]),
                     reads=[("xt", b)], writes=[("xn", nb), ("st1", t)])
                if XPL < 2:
                    continue
                S.op("act", lambda e, t=t: e.activation(out=st1[:, 8 + t:9 + t], in_=st1[:, t:t + 1], func=AF.Sqrt, scale=1.0 / 4096, bias=epsc),
                     reads=[("st1", t), ("epsc",)], writes=[("st1", 8 + t)])
                S.op("dve", lambda e, t=t: e.reciprocal(out=st1[:, 16 + t:17 + t], in_=st1[:, 8 + t:9 + t]),
                     reads=[("st1", 8 + t)], writes=[("st1", 16 + t)])
                if XPL < 3:
                    continue
                S.op("dve", lambda e, b=b, nb=nb, t=t: e.tensor_scalar(out=xn[:, nb, :], in0=xt[:, b, :], scalar1=st1[:, 16 + t:17 + t],
                                                                       scalar2=None, op0=ALU.mult),
                     reads=[("xt", b), ("st1", 16 + t)], writes=[("xn", nb)])
                if XPL < 4:
                    continue
                for k8 in range(4):
                    bank = 4 + ring("xtp", 2)
                    for kk in range(8):
                        kc = k8 * 8 + kk
                        S.op("pe", lambda e, bank=bank, kk=kk, kc=kc, nb=nb: e.transpose(
                            out=PSB(bank)[:, kk * 128:(kk + 1) * 128], in_=xn[:, nb, kc * 128:(kc + 1) * 128], identity=identb),
                            reads=[("xn", nb), ("identb",)], writes=[pk(bank)])
                    if XPL < 5:
                        continue
                    for kk in range(8):
                        kc = k8 * 8 + kk
                        src = PSB(bank)[:, kk * 128:(kk + 1) * 128]
                        dst = hT[:, kc, t * 128:(t + 1) * 128]
                        if bank == 4:
                            S.op("dve", lambda e, src=src, dst=dst, kc=kc: e.tensor_scalar(
                                out=dst, in0=src, scalar1=a_sc[:, kc:kc + 1], scalar2=shiftT[:, kc:kc + 1],
                                op0=ALU.mult, op1=ALU.add),
                                reads=[pk(bank), ("a_sc",), ("modT",)], writes=[("hT", kc, t)])
                        else:
                            S.op("act", lambda e, src=src, dst=dst, kc=kc: e.activation(
                                out=dst, in_=src, func=AF.Identity, scale=a_sc[:, kc:kc + 1], bias=shiftT[:, kc:kc + 1]),
                                reads=[pk(bank), ("a_sc",), ("modT",)], writes=[("hT", kc, t)])
            A.free("xt", "xn", "st1")

        deferred = []

        def flush_deferred():
            while deferred:
                deferred.pop(0)()

        def project(hT, col0, ncols, pieces, consumer, inter=None, every=2):
            b = load_w(win_v, col0, ncols)
            for p in pieces:
                bank = (0, 1, 7)[ring("projps", 3)]
                for kc in range(32):
                    S.op("pe", lambda e, b=b, kc=kc, p=p, bank=bank: e.matmul(
                        PS(bank)[0:ncols, :], lhsT=wt[:, b, kc, 0:ncols], rhs=hT[:, kc, p * 512:(p + 1) * 512],
                        start=(kc == 0), stop=(kc == 31)),
                        reads=[("wt", b)] + [("hT", kc, 4 * p + tt) for tt in range(4)], writes=[pk(bank)])
                    if inter and kc % every == every - 1:
                        inter.pop(0)()
                flush_deferred()
                consumer(p, bank)

        kT = A.alloc("kT", BF16, [4, 2048]); Vsb = A.alloc("Vsb", BF16, [16, 512]); kIT = A.alloc("kIT", BF16, [2048])
        tmpb = A.alloc("tmpb", BF16, [3, 512]); tmpf = A.alloc("tmpf", F32, [2, 512])

        def rms_feat(bank, gcol, gkey, dst, dkey, extra_reads=()):
            tb = ring("tmpb", 3)
            S.op("act", lambda e: e.activation(out=tmpb[:, tb, :], in_=PS(bank), func=AF.Square),
                 reads=[pk(bank)], writes=[("tmpb", tb)])
            deferred.append(lambda: rms_feat_b(bank, gcol, gkey, dst, dkey, tb))

        def rms_feat_b(bank, gcol, gkey, dst, dkey, tb):
            S.op("pe", lambda e: e.matmul(PS(2), lhsT=onesb, rhs=tmpb[:, tb, :], start=True, stop=True),
                 reads=[("tmpb", tb), ("onesb",)], writes=[pk(2)])
            fb = ring("tmpf", 2)
            S.op("act", lambda e: e.activation(out=tmpf[:, fb, :], in_=PS(2), func=AF.Sqrt, bias=epsc128),
                 reads=[pk(2), ("epsc",)], writes=[("tmpf", fb)])
            S.op("dve", lambda e: e.reciprocal(out=tmpf[:, fb, :], in_=tmpf[:, fb, :]), reads=[("tmpf", fb)], writes=[("tmpf", fb)])
            S.op("dve", lambda e: e.scalar_tensor_tensor(out=dst, in0=PS(bank).rearrange("p (a b) -> p a b", b=128) if len(dst.shape) == 3 else PS(bank),
                                                         scalar=gcol,
                                                         in1=tmpf[:, fb, :].rearrange("p (a b) -> p a b", b=128) if len(dst.shape) == 3 else tmpf[:, fb, :],
                                                         op0=ALU.mult, op1=ALU.mult),
                 reads=[pk(bank), ("tmpf", fb), gkey], writes=[dkey])

        def kv_pass(hT, L0):
            def c_ki(p, bank):
                S.op("act", lambda e: e.activation(out=kIT[:, L0 + p * 512: L0 + (p + 1) * 512], in_=PS(bank), func=AF.Copy),
                     reads=[pk(bank)], writes=[("kIT", (L0 // 512) + p)])
            project(hT, C_KI, 128, [0, 1], c_ki)
            for g in range(4):
                def c_k(p, bank, g=g):
                    rms_feat(bank, qkg[:, 1:2], ("qkg",), kT[:, g, L0 + p * 512: L0 + (p + 1) * 512], ("kT", g, (L0 // 512) + p))
                project(hT, C_K + g * 128, 128, [0, 1], c_k)
            for g in range(4):
                def c_v(p, bank, g=g):
                    tb = ring("tmpb", 3)
                    S.op("act", lambda e: e.activation(out=tmpb[:, tb, :], in_=PS(bank), func=AF.Copy),
                         reads=[pk(bank)], writes=[("tmpb", tb)])
                    def part_b(tb=tb, p=p, g=g):
                        for tt in range(4):
                            S.op("pe", lambda e, tt=tt: e.transpose(out=PSB(6)[:, tt * 128:(tt + 1) * 128],
                                                                    in_=tmpb[:, tb, tt * 128:(tt + 1) * 128], identity=identb),
                                 reads=[("tmpb", tb), ("identb",)], writes=[pk(6)])
                        Lt = L0 // 128 + p * 4
                        S.op("dve", lambda e: e.tensor_copy(out=Vsb[:, Lt:Lt + 4, g * 128:(g + 1) * 128],
                                                            in_=PSB(6)[:, 0:512].rearrange("p (a b) -> p a b", b=128)),
                             reads=[pk(6)], writes=[("Vsb", g, Lt // 4)])
                    deferred.append(part_b)
                project(hT, C_VB + g * 128, 128, [0, 1], c_v)
            flush_deferred()

        S.phase = "xkv_other"
        hT = A.alloc("hT", BF16, [32, 1024])
        x_pass(1024, hT)
        if os.environ.get("K_STOP", "") == "xp" and XPL >= 5:
            dbg_out("hT", hT, [128, 32, 1024], [("hT", kc, t) for kc in range(32) for t in range(8)], dt=BF16)
        ckpt("xp")
        kv_pass(hT, 1024)
        S.phase = "xkv_own"
        x_pass(0, hT)
        kv_pass(hT, 0)
        if dbg:
            dbg_out("hT", hT, [128, 32, 1024], [("hT", kc, t) for kc in range(32) for t in range(8)], dt=BF16)
            dbg_out("kT", kT, [128, 4, 2048], [("kT", g, p) for g in range(4) for p in range(4)], dt=BF16)
            dbg_out("Vsb", Vsb, [128, 16, 512], [("Vsb", g, p) for g in range(4) for p in range(4)], dt=BF16)
            dbg_out("kIT", kIT, [128, 2048], [("kIT", p) for p in range(4)], dt=BF16)
        ckpt("kv")

        S.phase = "indexer"
        sc = A.alloc("sc", F32, [9216])
        wI = A.alloc("wI", F32, [8, 32]); wItmp = A.alloc("wItmp", F32, [2, 512])

        def c_wi(p, bank):
            S.op("act", lambda e: e.activation(out=wItmp[0:32, p, :], in_=PS(bank)[0:32, :], func=AF.Copy, scale=1.0 / 64.0),
                 reads=[pk(bank)], writes=[("wItmp", p)])
            for tt in range(4):
                S.op("pe", lambda e, tt=tt: e.transpose(out=PS(6)[:, tt * 32:(tt + 1) * 32],
                                                        in_=wItmp[0:32, p, tt * 128:(tt + 1) * 128], identity=identf[0:32, 0:32]),
                     reads=[("wItmp", p), ("identf",)], writes=[pk(6)])
            S.op("dve", lambda e: e.tensor_copy(out=wI[:, p * 4:(p + 1) * 4, :], in_=PS(6)[:, 0:128].rearrange("p (a b) -> p a b", b=32)),
                 reads=[pk(6)], writes=[("wI", p)])
        project(hT, C_WI, 32, [0, 1], c_wi)

        qIT = A.alloc("qIT", BF16, [2, 1024])
        soff = [128 * i * (i + 1) for i in range(8)]
        def dot_thunks(h, qb):
            th = []
            for i in range(8):
                w_i = 128 * (i + 1)
                for part in range(2):
                    for c0 in range(0, w_i, 512):
                        def thunk(i=i, w_i=w_i, part=part, c0=c0):
                            cw = min(512, w_i - c0)
                            bank = 3 + ring("dotps", 2)
                            kcol = part * 1024 + c0
                            S.op("pe", lambda e: e.matmul(
                                PS(bank)[:, 0:cw], lhsT=qIT[:, qb, i * 128:(i + 1) * 128], rhs=kIT[:, kcol:kcol + cw],
                                start=True, stop=True),
                                reads=[("qIT", qb, i // 4)] + [("kIT", q) for q in range(kcol // 512, (kcol + cw - 1) // 512 + 1)],
                                writes=[pk(bank)])
                            tb = ring("tmpb", 3)
                            S.op("act", lambda e: e.activation(out=tmpb[:, tb, 0:cw], in_=PS(bank)[:, 0:cw], func=AF.Relu),
                                 reads=[pk(bank)], writes=[("tmpb", tb)])
                            o0 = soff[i] + part * w_i + c0
                            wcol = wI[:, i, h:h + 1]
                            skey = ("sc", i, part, c0)
                            if h == 0:
                                S.op("dve", lambda e: e.tensor_scalar(
                                    out=sc[:, o0:o0 + cw], in0=tmpb[:, tb, 0:cw], scalar1=wcol, scalar2=None, op0=ALU.mult),
                                    reads=[("tmpb", tb), ("wI", i // 4)], writes=[skey])
                            else:
                                S.op("dve", lambda e: e.scalar_tensor_tensor(
                                    out=sc[:, o0:o0 + cw], in0=tmpb[:, tb, 0:cw], scalar=wcol, in1=sc[:, o0:o0 + cw],
                                    op0=ALU.mult, op1=ALU.add),
                                    reads=[("tmpb", tb), ("wI", i // 4), skey], writes=[skey])
                        th.append(thunk)
            return th

        pending = []
        for h in range(32):
            qb = ring("qIT", 2)

            def c_qi(p, bank, qb=qb):
                S.op("act", lambda e: e.activation(out=qIT[:, qb, p * 512:(p + 1) * 512], in_=PS(bank), func=AF.Copy),
                     reads=[pk(bank)], writes=[("qIT", qb, p)])
            project(hT, C_QI + h * 128, 128, [0, 1], c_qi, inter=pending, every=2)
            while pending:
                pending.pop(0)()
            pending = dot_thunks(h, qb)
        while pending:
            pending.pop(0)()
        sc_keys = [("sc", i, part, c0) for i in range(8) for part in range(2) for c0 in range(0, 128 * (i + 1), 512)]

        def sck(i):
            return [k for k in sc_keys if k[1] == i]
        A.free("qIT", "wItmp", "kIT")
        dbg_out("sc", sc, [128, 9216], sc_keys)
        ckpt("sc")

        S.phase = "topk_qproj"
        qT = A.alloc("qT", BF16, [8, 16, 128])
        junk = A.alloc("junk", BF16, [2048]); tk = A.alloc("tk", F32, [64])
        selv = sc.bitcast(BF16)
        lo = tk[:, 0:8]; w0 = tk[:, 8:16]; mid = tk[:, 16:24]; cnt = tk[:, 24:32]; pw = tk[:, 32:40]
        hi = tk[:, 40:48]; t1 = tk[:, 48:56]; t2 = tk[:, 56:64]

        def q_head(h):
            def c_q(p, bank, h=h):
                rms_feat(bank, gqs, ("gqs",), qT[:, p * 4:(p + 1) * 4, h, :], ("qT", h, p))
            project(hT, C_Q + h * 128, 128, [0, 1], c_q)

        S.op("dve", lambda e: e.memset(tk, 0.0), writes=[("tk",)])
        S.op("dve", lambda e: e.memset(lo[:, 0:1], -1.0e29), reads=[("tk",)], writes=[("tk",)])
        for i in range(8):
            w_i = 128 * (i + 1)
            o = soff[i]
            S.op("dve", lambda e, o=o, w_i=w_i: e.tensor_tensor(out=sc[:, o + w_i - 128:o + w_i], in0=sc[:, o + w_i - 128:o + w_i],
                                                                in1=cmask[:, 0:128], op=ALU.add),
                 reads=sck(i) + [("cmask",)], writes=sck(i))
            S.op("dve", lambda e, o=o, w_i=w_i: e.tensor_tensor(out=sc[:, o + 2 * w_i - 128:o + 2 * w_i], in0=sc[:, o + 2 * w_i - 128:o + 2 * w_i],
                                                                in1=cmask[:, 128:256], op=ALU.add),
                 reads=sck(i) + [("cmask",)], writes=sck(i))
            if i >= 1:
                S.op("dve", lambda e, o=o, w_i=w_i, i=i: e.tensor_reduce(out=hi[:, i:i + 1], in_=sc[:, o:o + 2 * w_i], axis=AX.X, op=ALU.max),
                     reads=sck(i) + [("tk",)], writes=[("tk",)])
                S.op("dve", lambda e, o=o, w_i=w_i, i=i: e.tensor_reduce(out=t1[:, i:i + 1], in_=sc[:, o:o + w_i - 128], axis=AX.X, op=ALU.min),
                     reads=sck(i) + [("tk",)], writes=[("tk",)])
                S.op("dve", lambda e, o=o, w_i=w_i, i=i: e.tensor_reduce(out=t2[:, i:i + 1], in_=sc[:, o + w_i:o + 2 * w_i - 128], axis=AX.X, op=ALU.min),
                     reads=sck(i) + [("tk",)], writes=[("tk",)])
        S.op("dve", lambda e: e.tensor_tensor(out=lo[:, 1:8], in0=t1[:, 1:8], in1=t2[:, 1:8], op=ALU.min), reads=[("tk",)], writes=[("tk",)])
        S.op("dve", lambda e: e.tensor_tensor(out=w0[:, 1:8], in0=hi[:, 1:8], in1=lo[:, 1:8], op=ALU.subtract), reads=[("tk",)], writes=[("tk",)])
        q_next = 0
        for it in range(NIT):
            f = 2.0 ** (-(it + 1))
            S.op("dve", lambda e, f=f: e.scalar_tensor_tensor(out=mid, in0=w0, scalar=f, in1=lo, op0=ALU.mult, op1=ALU.add),
                 reads=[("tk",)], writes=[("tk",)])
            S.op("dve", lambda e: e.memset(cnt, 0.0), reads=[("tk",)], writes=[("tk",)])
            for i in range(1, 8):
                w_i = 128 * (i + 1)
                o = soff[i]
                S.op("dve", lambda e, o=o, w_i=w_i, i=i: e.tensor_scalar(out=junk[:, 0:2 * w_i], in0=sc[:, o:o + 2 * w_i], scalar1=mid[:, i:i + 1],
                                                                         scalar2=0.0, op0=ALU.is_ge, op1=ALU.add, accum_out=cnt[:, i:i + 1]),
                     reads=sck(i) + [("tk",)], writes=[("junk",), ("tk",)])
            S.op("dve", lambda e: e.scalar_tensor_tensor(out=pw, in0=cnt, scalar=255.5, in1=w0, op0=ALU.is_ge, op1=ALU.mult),
                 reads=[("tk",)], writes=[("tk",)])
            S.op("dve", lambda e, f=f: e.scalar_tensor_tensor(out=lo, in0=pw, scalar=f, in1=lo, op0=ALU.mult, op1=ALU.add),
                 reads=[("tk",)], writes=[("tk",)])
            if q_next < 16:
                q_head(q_next)
                q_next += 1
        while q_next < 16:
            q_head(q_next)
            q_next += 1
        flush_deferred()
        A.free("hT")
        sel = A.alloc("sel", BF16, [9216])
        for i in range(8):
            w_i = 128 * (i + 1)
            o = soff[i]
            S.op("dve", lambda e, o=o, w_i=w_i, i=i: e.tensor_scalar(out=sel[:, o:o + 2 * w_i], in0=sc[:, o:o + 2 * w_i], scalar1=lo[:, i:i + 1],
                                                                     scalar2=None, op0=ALU.is_ge),
                 reads=sck(i) + [("tk",)], writes=[("sel", i)])
        dbg_out("tk", tk, [128, 64], [("tk",)])
        ckpt("tk")
        S.phase = "selT_bias"
        selT = A.alloc("selT", BF16, [72, 128])
        toff = [i * (i + 1) for i in range(8)]
        for i in range(8):
            nt = 2 * (i + 1)
            for j0 in range(0, nt, 8):
                jn = min(8, nt - j0)
                bank = 4 + ring("xtp", 2)
                for jj in range(jn):
                    col = soff[i] + (j0 + jj) * 128
                    S.op("pe", lambda e, bank=bank, jj=jj, col=col: e.transpose(out=PSB(bank)[:, jj * 128:(jj + 1) * 128],
                                                                               in_=sel[:, col:col + 128], identity=identb),
                         reads=[("sel", i), ("identb",)], writes=[pk(bank)])
                t0 = toff[i] + j0
                S.op("dve", lambda e, bank=bank, jn=jn, t0=t0: e.tensor_copy(out=selT[:, t0:t0 + jn, :],
                                                                            in_=PSB(bank)[:, 0:jn * 128].rearrange("p (a b) -> p a b", b=128)),
                     reads=[pk(bank)], writes=[("selT", i, j0)])
        A.free("sc", "junk", "tk", "sel")

        EB = A.alloc("EB", BF16, [3, 16, 128])
        rbs = A.alloc("rbs", F32, [16]); ohs = A.alloc("ohs", F32, [383]); vecs = A.alloc("vecs", F32, [383])
        Hk = A.alloc("Hk", F32, [2, 16, 128])
        S.dma("sp", S.chan("ld_rbs"), rbs[0:32, :], rb_d, writes=[("rbs",)])
        S.dma("sp", S.chan("ld_ohs"), ohs[0:32, :], oh_d, writes=[("ohs",)])
        S.op("pe", lambda e: e.matmul(PS(0)[0:16, 0:383], lhsT=rbs[0:32, :], rhs=ohs[0:32, :], start=True, stop=True),
             reads=[("rbs",), ("ohs",)], writes=[pk(0)])
        S.op("dve", lambda e: e.tensor_copy(out=c31[0:16, :], in_=PS(0)[0:16, 0:1]), reads=[pk(0)], writes=[("c31",)])
        S.op("dve", lambda e: e.tensor_scalar(out=vecs[0:16, :], in0=PS(0)[0:16, 0:383], scalar1=c31[0:16, :],
                                              scalar2=None, op0=ALU.subtract), reads=[pk(0), ("c31",)], writes=[("vecs",)])
        scc = S.chan("scc")
        S.dma("sp", scc, scr_d, vecs[0:16, :], reads=[("vecs",)], writes=[("scr",)])
        for m in range(2):
            base = 0 if m == 1 else 128
            src = bass.AP(tensor=scr_t, offset=base, ap=[[1, 128], [383, 16], [1, 128]])
            S.dma("sp", S.chan("ld_hk%d" % m), Hk[:, m, :, :], src, reads=[("scr",)], writes=[("Hk", m)])
        for m in range(2):
            for hq in range(4):
                bank = 1 + (m * 4 + hq) % 2
                for hh in range(4):
                    h = hq * 4 + hh
                    S.op("pe", lambda e, bank=bank, m=m, h=h, hh=hh: e.matmul(
                        PS(bank)[:, hh * 128:(hh + 1) * 128], lhsT=Hk[:, m, h, :], rhs=Jf, start=True, stop=True),
                        reads=[("Hk", m), ("Jf",)], writes=[pk(bank)])
                dst = EB[:, 0 if m == 0 else 1, hq * 4:(hq + 1) * 4, :]
                S.op("act", lambda e, bank=bank, dst=dst: e.activation(
                    out=dst, in_=PS(bank).rearrange("p (a b) -> p a b", a=4), func=AF.Exp),
                    reads=[pk(bank)], writes=[("EB", m, hq)])
                if m == 1:
                    dst2 = EB[:, 2, hq * 4:(hq + 1) * 4, :]
                    S.op("act", lambda e, bank=bank, dst2=dst2: e.activation(
                        out=dst2, in_=PS(bank).rearrange("p (a b) -> p a b", a=4), func=AF.Exp, scale=flag0),
                        reads=[pk(bank), ("flag0",)], writes=[("EB", 2, hq)])
        A.free("rbs", "ohs", "vecs", "Hk")

        S.phase = "attn"
        oT = A.alloc("catB", BF16, [16, 1024], top=True)
        Eb = A.alloc("Eb", BF16, [5, 512]); Pb = A.alloc("Pb", BF16, [5, 512]); rden = A.alloc("rden", F32, [2, 512])
        def attention_group(g):
            items = []
            for i in range(8):
                tl = [(j, j, 0 if j == i else None) for j in range(i + 1)]
                tl += [(8 + j, (i + 1) + j, 1 if j == i else (2 if j == i - 1 else None)) for j in range(i + 1)]
                for n, (L, tj, kind) in enumerate(tl):
                    items.append((i, L, tj, kind, n == 0, n == len(tl) - 1))
            pend = []

            def do_pv(it):
                (i, L, tj, kind, first, last, pbuf) = it
                ob = 2 + (i % 2) * 2
                S.op("pe", lambda e: e.matmul(PS(ob), lhsT=Vsb[:, L, g * 128:(g + 1) * 128], rhs=Pb[:, pbuf, :], start=first, stop=last),
                     reads=[("Pb", pbuf), ("Vsb", g, L // 4)], writes=[pk(ob)])
                S.op("pe", lambda e: e.matmul(PS(ob + 1), lhsT=onesb, rhs=Pb[:, pbuf, :], start=first, stop=last),
                     reads=[("Pb", pbuf), ("onesb",)], writes=[pk(ob + 1)])
                if last:
                    rb = ring("rden", 2)
                    S.op("dve", lambda e: e.reciprocal(out=rden[:, rb, :], in_=PS(ob + 1)), reads=[pk(ob + 1)], writes=[("rden", rb)])
                    S.op("dve", lambda e: e.tensor_tensor(out=oT[:, g * 4:(g + 1) * 4, i * 128:(i + 1) * 128],
                                                          in0=PS(ob).rearrange("p (a b) -> p a b", b=128),
                                                          in1=rden[:, rb, :].rearrange("p (a b) -> p a b", b=128), op=ALU.mult),
                         reads=[pk(ob), ("rden", rb)], writes=[("catB", g, i)])

            for (i, L, tj, kind, first, last) in items:
                lb = (0, 1, 6, 7)[ring("lgps", 4)]
                S.op("pe", lambda e, lb=lb, L=L, i=i: e.matmul(PS(lb), lhsT=kT[:, g, L * 128:(L + 1) * 128],
                                                               rhs=qT[:, i, g * 4:(g + 1) * 4, :], start=True, stop=True),
                     reads=[("kT", g, L // 4)] + [("qT", g * 4 + r, i // 4) for r in range(4)], writes=[pk(lb)])
                eb = ring("Eb", 5)
                S.op("act", lambda e, lb=lb, eb=eb: e.activation(out=Eb[:, eb, :], in_=PS(lb), func=AF.Exp),
                     reads=[pk(lb)], writes=[("Eb", eb)])
                pbuf = ring("Pb", 5)
                mask = selT[:, toff[i] + tj, :]
                mask_b = mask.unsqueeze(1).broadcast_to([128, 4, 128])
                selk = ("selT", i, (tj // 8) * 8)
                S.op("dve", lambda e, eb=eb, pbuf=pbuf, mask_b=mask_b: e.tensor_tensor(
                    out=Pb[:, pbuf, :].rearrange("p (a b) -> p a b", b=128), in0=Eb[:, eb, :].rearrange("p (a b) -> p a b", b=128),
                    in1=mask_b, op=ALU.mult), reads=[("Eb", eb), selk], writes=[("Pb", pbuf)])
                if kind is not None:
                    ebt = EB[:, kind, g * 4:(g + 1) * 4, :]
                    S.op("dve", lambda e, pbuf=pbuf, ebt=ebt: e.tensor_tensor(
                        out=Pb[:, pbuf, :].rearrange("p (a b) -> p a b", b=128), in0=Pb[:, pbuf, :].rearrange("p (a b) -> p a b", b=128),
                        in1=ebt, op=ALU.mult), reads=[("Pb", pbuf), ("EB", min(kind, 1), g), ("EB", 2, g)], writes=[("Pb", pbuf)])
                pend.append((i, L, tj, kind, first, last, pbuf))
                if len(pend) > 3:
                    do_pv(pend.pop(0))
            while pend:
                do_pv(pend.pop(0))

        for g in range(4):
            attention_group(g)
        if dbg:
            dbg_out("qT", qT, [128, 8, 16, 128], [("qT", h, p) for h in range(16) for p in range(2)], dt=BF16)
            dbg_out("EB", EB, [128, 3, 16, 128], [("EB", m, hq) for m in range(3) for hq in range(4)], dt=BF16)
            dbg_out("selT", selT, [128, 72, 128], [("selT", i, j0) for i in range(8) for j0 in range(0, 2 * (i + 1), 8)], dt=BF16)
        A.free("kT", "Vsb", "selT", "qT", "EB", "Eb", "Pb", "rden", "wI")
        dbg_out("oT", oT, [128, 16, 1024], [("catB", g, i) for g in range(4) for i in range(8)], dt=BF16)
        ckpt("att")

        def sumsq_feat(src, skeys_fn, nchunks, banks, dstname):
            dst = A.alloc(dstname, F32, [1024])
            for c in range(nchunks):
                for p in range(2):
                    tb = ring("tmpb", 3)
                    S.op("dve", lambda e, c=c, p=p, tb=tb: e.tensor_tensor(out=tmpb[:, tb, :], in0=src[:, c, p * 512:(p + 1) * 512],
                                                                            in1=src[:, c, p * 512:(p + 1) * 512], op=ALU.mult),
                         reads=skeys_fn(c, p), writes=[("tmpb", tb)])
                    S.op("pe", lambda e, c=c, p=p, tb=tb: e.matmul(PS(banks[p]), lhsT=onesb, rhs=tmpb[:, tb, :], start=(c == 0), stop=(c == nchunks - 1)),
                         reads=[("tmpb", tb), ("onesb",)], writes=[pk(banks[p])])
            for p in range(2):
                S.op("act", lambda e, p=p: e.activation(out=dst[:, p * 512:(p + 1) * 512], in_=PS(banks[p]), func=AF.Sqrt,
                                                        scale=1.0 / (128 * nchunks), bias=epsc), reads=[pk(banks[p]), ("epsc",)], writes=[(dstname, p)])
                S.op("dve", lambda e, p=p: e.reciprocal(out=dst[:, p * 512:(p + 1) * 512], in_=dst[:, p * 512:(p + 1) * 512]),
                     reads=[(dstname, p)], writes=[(dstname, p)])
            return dst

        S.phase = "rmsB_xpass2"
        rstdB = sumsq_feat(oT, lambda c, p: [("catB", c // 4, i) for i in range(4 * p, 4 * p + 4)], 16, [2, 3], "rstdB")

        hT = A.alloc("hT2", BF16, [32, 1024])
        hT2 = hT

        def x_pass2(tok0, hTb):
            xt = A.alloc("xt", F32, [2, 4096]); xn = A.alloc("xn", BF16, [2, 4096])
            st1 = A.alloc("st1", F32, [32])
            S.op("dve", lambda e: e.memset(st1, 0.0), writes=[("st1", q) for q in range(24)])
            for t in range(8):
                b = ring("xt", 2)
                nb = ring("xn", 2)
                S.dma("sp", xch[b], xt[:, b, :], x_d[tok0 + t * 128: tok0 + (t + 1) * 128, :], writes=[("xt", b)])
                S.op("act", lambda e, b=b, t=t, nb=nb: e.activation(out=xn[:, nb, :], in_=xt[:, b, :], func=AF.Square, accum_out=st1[:, t:t + 1]),
                     reads=[("xt", b)], writes=[("xn", nb), ("st1", t)])
                S.op("act", lambda e, t=t: e.activation(out=st1[:, 8 + t:9 + t], in_=st1[:, t:t + 1], func=AF.Sqrt, scale=1.0 / 4096, bias=epsc),
                     reads=[("st1", t), ("epsc",)], writes=[("st1", 8 + t)])
                S.op("dve", lambda e, t=t: e.reciprocal(out=st1[:, 16 + t:17 + t], in_=st1[:, 8 + t:9 + t]),
                     reads=[("st1", 8 + t)], writes=[("st1", 16 + t)])
                S.op("dve", lambda e, b=b, nb=nb, t=t: e.tensor_scalar(out=xn[:, nb, :], in0=xt[:, b, :], scalar1=st1[:, 16 + t:17 + t],
                                                                       scalar2=None, op0=ALU.mult),
                     reads=[("xt", b), ("st1", 16 + t)], writes=[("xn", nb)])
                for k8 in range(4):
                    bank = 4 + ring("xtp", 2)
                    for kk in range(8):
                        kc = k8 * 8 + kk
                        S.op("pe", lambda e, bank=bank, kk=kk, kc=kc, nb=nb: e.transpose(
                            out=PSB(bank)[:, kk * 128:(kk + 1) * 128], in_=xn[:, nb, kc * 128:(kc + 1) * 128], identity=identb),
                            reads=[("xn", nb), ("identb",)], writes=[pk(bank)])
                    for kk in range(8):
                        kc = k8 * 8 + kk
                        src = PSB(bank)[:, kk * 128:(kk + 1) * 128]
                        dst = hTb[:, kc, t * 128:(t + 1) * 128]
                        if bank == 4:
                            S.op("dve", lambda e, src=src, dst=dst, kc=kc: e.tensor_scalar(
                                out=dst, in0=src, scalar1=a_sc[:, kc:kc + 1], scalar2=shiftT[:, kc:kc + 1], op0=ALU.mult, op1=ALU.add),
                                reads=[pk(bank), ("a_sc",), ("modT",)], writes=[("hT2", kc, t)])
                        else:
                            S.op("act", lambda e, src=src, dst=dst, kc=kc: e.activation(
                                out=dst, in_=src, func=AF.Identity, scale=a_sc[:, kc:kc + 1], bias=shiftT[:, kc:kc + 1]),
                                reads=[pk(bank), ("a_sc",), ("modT",)], writes=[("hT2", kc, t)])
            A.free("xt", "xn", "st1")

        x_pass2(0, hT2)

        def project2(col0, ncols, pieces, consumer):
            b = load_w(win_v, col0, ncols)
            for p in pieces:
                bank = (0, 1, 7)[ring("projps", 3)]
                for kc in range(32):
                    S.op("pe", lambda e, b=b, kc=kc, p=p, bank=bank: e.matmul(
                        PS(bank)[0:ncols, :], lhsT=wt[:, b, kc, 0:ncols], rhs=hT2[:, kc, p * 512:(p + 1) * 512],
                        start=(kc == 0), stop=(kc == 31)),
                        reads=[("wt", b)] + [("hT2", kc, 4 * p + tt) for tt in range(4)], writes=[pk(bank)])
                consumer(p, bank)

        def gate_consumer(cat, cname, rstd, rname, gcol, gkey, c):
            def cons(p, bank):
                tb = ring("tmpb", 3)
                S.op("act", lambda e: e.activation(out=tmpb[:, tb, :], in_=PS(bank), func=AF.Silu), reads=[pk(bank)], writes=[("tmpb", tb)])
                tb2 = ring("tmpb", 3)
                ck = [(cname, c // 4, i) for i in range(4 * p, 4 * p + 4)] if cname == "catB" else [(cname, c, p)]
                S.op("dve", lambda e: e.tensor_tensor(out=tmpb[:, tb2, :], in0=cat[:, c, p * 512:(p + 1) * 512], in1=rstd[:, p * 512:(p + 1) * 512],
                                                      op=ALU.mult), reads=ck + [(rname, p)], writes=[("tmpb", tb2)])
                S.op("dve", lambda e: e.scalar_tensor_tensor(out=cat[:, c, p * 512:(p + 1) * 512], in0=tmpb[:, tb2, :], scalar=gcol,
                                                             in1=tmpb[:, tb, :], op0=ALU.mult, op1=ALU.mult),
                     reads=[("tmpb", tb2), ("tmpb", tb), gkey], writes=ck)
            return cons

        S.phase = "gB"
        for h in range(16):
            project2(C_GB + h * 128, 128, [0, 1], gate_consumer(oT, "catB", rstdB, "rstdB", onT[:, 16 + h:17 + h], ("onT",), h))
        A.free("rstdB")
        dbg_out("catB", oT, [128, 16, 1024], [("catB", g, i) for g in range(4) for i in range(8)], dt=BF16)

        S.phase = "A_v"
        catA = A.alloc("catA", BF16, [16, 1024])
        wsn = A.alloc("wsn", F32, [16, 128]); wsT = A.alloc("wsT", BF16, [16, 128]); bsb = A.alloc("bsb", F32, [16, 128])
        S.dma("sp", S.chan("ld_wsn"), wsn, ws_d.rearrange("c t s -> t c s"), writes=[("wsn",)])
        S.dma("sp", S.chan("ld_bsb"), bsb.rearrange("p a b -> p (a b)"), bs_d.partition_broadcast(128), writes=[("bsb",)])
        for c4 in range(4):
            bank = 4 + ring("xtp", 2)
            for cc in range(4):
                c = c4 * 4 + cc
                S.op("pe", lambda e, bank=bank, cc=cc, c=c: e.transpose(out=PS(bank)[:, cc * 128:(cc + 1) * 128], in_=wsn[:, c, :], identity=identf),
                     reads=[("wsn",), ("identf",)], writes=[pk(bank)])
            S.op("dve", lambda e, bank=bank, c4=c4: e.tensor_tensor(out=wsT[:, c4 * 4:(c4 + 1) * 4, :], in0=PS(bank).rearrange("p (a b) -> p a b", b=128),
                                                                    in1=trilT.unsqueeze(1).broadcast_to([128, 4, 128]), op=ALU.mult),
                 reads=[pk(bank), ("trilT",)], writes=[("wsT", c4)])
        A.free("wsn")
        GELU = AF.Gelu_apprx_tanh
        vdef = []
        GELU_COMP = os.environ.get("K_GELU", "lut") == "comp"

        def gelu_to(dst, dkey, bank):
            if not GELU_COMP:
                S.op("act", lambda e: e.activation(out=dst, in_=PS(bank), func=GELU), reads=[pk(bank)], writes=[dkey])
                return
            tf = ring("tmpf", 2)
            S.op("act", lambda e: e.activation(out=tmpf[:, tf, :], in_=PS(bank), func=AF.Square), reads=[pk(bank)], writes=[("tmpf", tf)])
            S.op("dve", lambda e: e.tensor_scalar(out=tmpf[:, tf, :], in0=tmpf[:, tf, :], scalar1=0.044715, scalar2=1.0, op0=ALU.mult, op1=ALU.add),
                 reads=[("tmpf", tf)], writes=[("tmpf", tf)])
            S.op("dve", lambda e: e.tensor_tensor(out=tmpf[:, tf, :], in0=tmpf[:, tf, :], in1=PS(bank), op=ALU.mult),
                 reads=[("tmpf", tf), pk(bank)], writes=[("tmpf", tf)])
            S.op("act", lambda e: e.activation(out=tmpf[:, tf, :], in_=tmpf[:, tf, :], func=AF.Sigmoid, scale=1.5957691216057308),
                 reads=[("tmpf", tf)], writes=[("tmpf", tf)])
            S.op("dve", lambda e: e.tensor_tensor(out=dst, in0=tmpf[:, tf, :], in1=PS(bank), op=ALU.mult),
                 reads=[("tmpf", tf), pk(bank)], writes=[dkey])
        for c in range(16):
            def c_v(p, bank, c=c):
                gelu_to(catA[:, c, p * 512:(p + 1) * 512], ("catA", c, p), bank)
                tb = ring("tmpb", 3)
                S.op("dve", lambda e: e.tensor_tensor(out=tmpb[:, tb, :], in0=catA[:, c, p * 512:(p + 1) * 512], in1=catA[:, c, p * 512:(p + 1) * 512],
                                                       op=ALU.mult), reads=[("catA", c, p)], writes=[("tmpb", tb)])
                while vdef:
                    vdef.pop(0)()

                def stat_mm(c=c, p=p, tb=tb):
                    S.op("pe", lambda e: e.matmul(PS(2 + p), lhsT=onesb, rhs=catA[:, c, p * 512:(p + 1) * 512], start=(c == 0), stop=(c == 15)),
                         reads=[("catA", c, p), ("onesb",)], writes=[pk(2 + p)])
                    S.op("pe", lambda e: e.matmul(PS(4 + p), lhsT=onesb, rhs=tmpb[:, tb, :], start=(c == 0), stop=(c == 15)),
                         reads=[("tmpb", tb), ("onesb",)], writes=[pk(4 + p)])
                vdef.append(stat_mm)
            project2(C_V + c * 128, 128, [0, 1], c_v)
        while vdef:
            vdef.pop(0)()
        S.phase = "A_main"
        meanv = A.alloc("meanv", F32, [1024]); rstdv = A.alloc("rstdv", F32, [1024])
        for p in range(2):
            sl = slice(p * 512, (p + 1) * 512)
            S.op("dve", lambda e, p=p, sl=sl: e.tensor_scalar(out=meanv[:, sl], in0=PS(2 + p), scalar1=1.0 / 2048, scalar2=None, op0=ALU.mult),
                 reads=[pk(2 + p)], writes=[("meanv", p)])
            S.op("dve", lambda e, p=p, sl=sl: e.tensor_scalar(out=rstdv[:, sl], in0=PS(4 + p), scalar1=1.0 / 2048, scalar2=EPS, op0=ALU.mult, op1=ALU.add),
                 reads=[pk(4 + p)], writes=[("rstdv", p)])
            tf = ring("tmpf", 2)
            S.op("dve", lambda e, sl=sl, tf=tf: e.tensor_tensor(out=tmpf[:, tf, :], in0=meanv[:, sl], in1=meanv[:, sl], op=ALU.mult),
                 reads=[("meanv", p)], writes=[("tmpf", tf)])
            S.op("dve", lambda e, sl=sl, tf=tf: e.tensor_tensor(out=rstdv[:, sl], in0=rstdv[:, sl], in1=tmpf[:, tf, :], op=ALU.subtract),
                 reads=[("rstdv", p), ("tmpf", tf)], writes=[("rstdv", p)])
            S.op("act", lambda e, sl=sl: e.activation(out=rstdv[:, sl], in_=rstdv[:, sl], func=AF.Sqrt),
                 reads=[("rstdv", p)], writes=[("rstdv", p)])
            S.op("dve", lambda e, sl=sl: e.reciprocal(out=rstdv[:, sl], in_=rstdv[:, sl]), reads=[("rstdv", p)], writes=[("rstdv", p)])
        vl = A.alloc("vl", F32, [2, 512]); vlnT = A.alloc("vlnT", BF16, [2, 1024]); vtok = A.alloc("vtok", BF16, [2, 8, 128])
        guT = A.alloc("guT", BF16, [2, 1024])
        def ln_dve(c):
            vb = c % 2
            for p in range(2):
                sl = slice(p * 512, (p + 1) * 512)
                lb = ring("vl", 2)
                S.op("dve", lambda e, sl=sl, lb=lb: e.tensor_tensor(out=vl[:, lb, :], in0=catA[:, c, sl], in1=meanv[:, sl], op=ALU.subtract),
                     reads=[("catA", c, p), ("meanv", p)], writes=[("vl", lb)])
                S.op("dve", lambda e, sl=sl, lb=lb: e.tensor_tensor(out=vl[:, lb, :], in0=vl[:, lb, :], in1=rstdv[:, sl], op=ALU.mult),
                     reads=[("vl", lb), ("rstdv", p)], writes=[("vl", lb)])
                S.op("dve", lambda e, sl=sl, lb=lb: e.tensor_scalar(out=vlnT[:, vb, sl], in0=vl[:, lb, :], scalar1=lnvT[:, c:c + 1],
                                                                    scalar2=lnvT[:, 16 + c:17 + c], op0=ALU.mult, op1=ALU.add),
                     reads=[("vl", lb), ("lnvT",)], writes=[("vlnT", vb, p)])

        def tr_pe(c):
            vb = c % 2
            for n in range(8):
                S.op("pe", lambda e, n=n: e.transpose(out=PSB(6)[:, n * 128:(n + 1) * 128], in_=vlnT[:, vb, n * 128:(n + 1) * 128], identity=identb),
                     reads=[("vlnT", vb, n // 4), ("identb",)], writes=[pk(6)])
            S.op("act", lambda e: e.activation(out=vtok[:, vb, :, :], in_=PSB(6).rearrange("p (a b) -> p a b", b=128), func=AF.Copy),
                 reads=[pk(6)], writes=[("vtok", vb)])

        def uproj(c):
            ub = c % 2

            def c_u(p, bank):
                gelu_to(guT[:, ub, p * 512:(p + 1) * 512], ("guT", ub, p), bank)
            project2(C_U + c * 128, 128, [0, 1], c_u)

        def tril(c):
            vb = c % 2
            ub = c % 2
            for p in range(2):
                for n4 in range(4):
                    n = p * 4 + n4
                    S.op("pe", lambda e, n=n, n4=n4, p=p: e.matmul(PS(2 + p)[:, n4 * 128:(n4 + 1) * 128], lhsT=vtok[:, vb, n, :], rhs=wsT[:, c, :],
                                                                   start=True, stop=True),
                         reads=[("vtok", vb), ("wsT", c // 4)], writes=[pk(2 + p)])
                tf = ring("tmpf", 2)
                S.op("dve", lambda e, p=p, tf=tf: e.tensor_tensor(out=tmpf[:, tf, :].rearrange("p (a b) -> p a b", b=128),
                                                                  in0=PS(2 + p).rearrange("p (a b) -> p a b", b=128),
                                                                  in1=bsb[:, c, :].unsqueeze(1).broadcast_to([128, 4, 128]), op=ALU.add),
                     reads=[pk(2 + p), ("bsb",)], writes=[("tmpf", tf)])
                S.op("dve", lambda e, p=p, tf=tf: e.tensor_tensor(out=catA[:, c, p * 512:(p + 1) * 512], in0=tmpf[:, tf, :],
                                                                  in1=guT[:, ub, p * 512:(p + 1) * 512], op=ALU.mult),
                     reads=[("tmpf", tf), ("guT", ub, p)], writes=[("catA", c, p)])

        ln_dve(0)
        tr_pe(0)
        for c in range(16):
            if c + 1 < 16:
                ln_dve(c + 1)
            uproj(c)
            if c + 1 < 16:
                tr_pe(c + 1)
            tril(c)
        A.free("meanv", "rstdv", "vl", "vlnT", "vtok", "guT", "wsT", "bsb")
        dbg_out("yA", catA, [128, 16, 1024], [("catA", c, p) for c in range(16) for p in range(2)], dt=BF16)
        ckpt("yA")
        S.phase = "gA"
        rstdA = sumsq_feat(catA, lambda c, p: [("catA", c, p)], 16, [4, 5], "rstdA")
        for c in range(16):
            project2(C_GA + c * 128, 128, [0, 1], gate_consumer(catA, "catA", rstdA, "rstdA", onT[:, c:c + 1], ("onT",), c))
        A.free("rstdA", "hT2", "wt", "tmpb", "tmpf")
        dbg_out("catA", catA, [128, 16, 1024], [("catA", c, p) for c in range(16) for p in range(2)], dt=BF16)

        S.phase = "outproj"
        gbc = A.alloc("gbc", F32, [4096]); dg = A.alloc("dg", F32, [2, 128])
        for kc in range(32):
            db = ring("dg", 2)
            S.op("dve", lambda e, db=db, kc=kc: e.tensor_scalar(out=dg[:, db, :], in0=identf, scalar1=gateT[:, kc:kc + 1], scalar2=None, op0=ALU.mult),
                 reads=[("identf",), ("modT",)], writes=[("dg", db)])
            bank = 4 + (kc // 4) % 2
            S.op("pe", lambda e, db=db, kc=kc, bank=bank: e.matmul(PS(bank)[:, (kc % 4) * 128:(kc % 4 + 1) * 128], lhsT=onesf, rhs=dg[:, db, :], start=True, stop=True),
                 reads=[("dg", db), ("onesf",)], writes=[pk(bank)])
            if kc % 4 == 3:
                S.op("act", lambda e, kc=kc, bank=bank: e.activation(out=gbc[:, (kc - 3) * 128:(kc + 1) * 128], in_=PS(bank), func=AF.Copy),
                     reads=[pk(bank)], writes=[("gbc", kc // 4)])
        wo = A.alloc("wo", BF16, [2, 32, 512])
        woch = [[S.chan(f"woch{i}_{h}") for h in range(2)] for i in range(2)]
        xr = A.alloc("xr", F32, [3, 512]); yo = A.alloc("yo", F32, [3, 512])
        xrch = [S.chan(f"xrch{i}") for i in range(3)]
        och = [S.chan(f"och{i}") for i in range(3)]
        wout_v = wout_d.rearrange("(kc p) n -> p kc n", p=128)
        outs = []
        for n in range(8):
            wb = ring("wo", 2)
            for half in range(2):
                S.dma("pool", woch[wb][half], wo[:, wb, half * 16:(half + 1) * 16, :], wout_v[:, half * 16:(half + 1) * 16, n * 512:(n + 1) * 512],
                      writes=[("wo", wb, half)])
            for t in range(8):
                bank = (0, 1, 7)[ring("projps", 3)]
                for kc in range(32):
                    src = catA[:, kc, t * 128:(t + 1) * 128] if kc < 16 else oT[:, kc - 16, t * 128:(t + 1) * 128]
                    rk = ("catA", kc, t // 4) if kc < 16 else ("catB", (kc - 16) // 4, t)
                    S.op("pe", lambda e, src=src, wb=wb, kc=kc, bank=bank: e.matmul(PS(bank), lhsT=src, rhs=wo[:, wb, kc, :], start=(kc == 0), stop=(kc == 31)),
                         reads=[rk, ("wo", wb, 0), ("wo", wb, 1)], writes=[pk(bank)])
                xb = ring("xr", 3)
                S.dma("sp", xrch[xb], xr[:, xb, :], x_d[t * 128:(t + 1) * 128, n * 512:(n + 1) * 512], writes=[("xr", xb)])
                yb = ring("yo", 3)
                S.op("dve", lambda e, bank=bank, yb=yb, n=n: e.tensor_tensor(out=yo[:, yb, :], in0=PS(bank), in1=gbc[:, n * 512:(n + 1) * 512], op=ALU.mult),
                     reads=[pk(bank), ("gbc", n)], writes=[("yo", yb)])
                S.op("dve", lambda e, yb=yb, xb=xb: e.tensor_tensor(out=yo[:, yb, :], in0=yo[:, yb, :], in1=xr[:, xb, :], op=ALU.add),
                     reads=[("yo", yb), ("xr", xb)], writes=[("yo", yb)])
                outs.append(S.dma("sp", och[yb], out_d[t * 128:(t + 1) * 128, n * 512:(n + 1) * 512], yo[:, yb, :], reads=[("yo", yb)]))
        S.emit(final_waits=outs[-3:] + list(dbg_d.values()))


def _t5_bucket(d):
    n = max(d, 0)
    if n < 16:
        return n
    nf = np.float32(max(n, 1))
    v = np.log(nf / np.float32(16)) / np.float32(math.log(128 / 16)) * np.float32(16)
    return min(16 + int(np.float32(v)), 31)


def _consts(j):
    ident = np.eye(128, dtype=np.float32)
    Jm = np.ascontiguousarray(ident[::-1])
    s = np.arange(128)[:, None]; t = np.arange(128)[None, :]
    trilT = (t >= s).astype(np.float32)
    cm = np.zeros((128, 256), np.float32)
    cm[:, 0:128] = np.where(np.arange(128)[None, :] <= np.arange(128)[:, None], 0.0, NEG)
    cm[:, 128:256] = 0.0 if j == 1 else NEG
    oh = np.zeros((32, 383), np.float32)
    for i in range(383):
        oh[_t5_bucket(255 - i), i] = 1.0
    flag = np.full((128, 1), 1.0 if j == 0 else 0.0, np.float32)
    return {"ident": ident, "Jm": Jm, "trilT": trilT, "cmask": cm, "ohrev": oh, "flag0": flag}


def _colT(v, n):
    return np.ascontiguousarray(v.reshape(n, 128).T)


_NC_CACHE = {}


def kernel(x, c, w_ada, b_ada, norm_g, w_in, ln_v_g, ln_v_b, w_s, b_s, q_norm_g, k_norm_g, rel_bias,
           out_norm_a_g, out_norm_b_g, w_out, _dbg=False):
    x = np.asarray(x, np.float32)
    f = lambda a: np.ascontiguousarray(np.asarray(a, np.float32))
    w_ada0, w_in0, w_out0 = f(w_ada[0]), f(w_in[0]), f(w_out[0])
    shared = {
        "w_ada": w_ada0, "w_in": w_in0, "w_out": w_out0,
        "b_adaT": _colT(f(b_ada[0]), 96), "norm_gT": _colT(f(norm_g[0]), 32),
        "lnvT": np.ascontiguousarray(np.concatenate([_colT(f(ln_v_g[0]), 16), _colT(f(ln_v_b[0]), 16)], axis=1)),
        "w_s": f(w_s[0]), "b_s": f(b_s[0]).reshape(1, 2048),
        "qk_g": np.ascontiguousarray(np.stack([f(q_norm_g[0]), f(k_norm_g[0])], axis=1)),
        "rel_bias": f(rel_bias),
        "onT": np.ascontiguousarray(np.concatenate([_colT(f(out_norm_a_g[0]), 16), _colT(f(out_norm_b_g[0]), 16)], axis=1)),
    }
    in_maps = []
    for core in range(8):
        b, j = core // 2, core % 2
        xb = x[b].reshape(16, 128, 4096)
        own = xb[j::2]; oth = xb[1 - j::2]
        xl = np.ascontiguousarray(np.concatenate([own, oth], axis=0).reshape(2048, 4096))
        m = dict(shared)
        m.update(_consts(j))
        m["x"] = xl
        m["cT"] = _colT(f(c[b]), 32)
        in_maps.append(m)
    key = bool(_dbg)
    if key not in _NC_CACHE:
        _NC_CACHE[key] = build_nc(dbg=key)
    nc = _NC_CACHE[key]
    ncores = int(os.environ.get("K_CORES", "8"))
    res = run_bass_kernel_spmd(nc, in_maps[:ncores], core_ids=list(range(ncores)))
    out = np.zeros((4, 16, 128, 4096), np.float32)
    for core in range(ncores):
        b, j = core // 2, core % 2
        out[b, j::2] = np.asarray(res.results[core]["out"], np.float32).reshape(8, 128, 4096)
    if _dbg:
        return out.reshape(4, 2048, 4096), res
    return out.reshape(4, 2048, 4096)
```
